# Optimizing a Trainium2 kernel written in Bass

```python
import jax, jax.numpy as jnp
from jax import lax
import numpy as np

D_MODEL = 1024
BATCH = 8
SEQ = 2048
DEPTH = 2
DEC_BATCH = 128
DEC_SEQ = 8
PAST_LEN = 16384
PAGE_SIZE = 128

N_MIXERS = 2
POOL_WINDOWS = (2, 4, 8, 16)
N_POOL_GROUPS = len(POOL_WINDOWS)
POOL_GROUP = D_MODEL // N_POOL_GROUPS
POOL_BUF = max(POOL_WINDOWS) - 1
CHUNK = 128
D_SGU = D_MODEL
N_SGU_GROUPS = 4
SGU_GROUP = D_SGU // N_SGU_GROUPS
D_FF = 4 * D_MODEL
EPS = 1e-6
N_POOL_LAYERS = (DEPTH + 1) // 2
N_SGU_LAYERS = DEPTH // 2

kernel_name = "pool_sgu_hybrid_decode_step"


def rms_norm(x, g):
    xf = x.astype(jnp.float32)
    y = xf * lax.rsqrt(jnp.mean(xf * xf, axis=-1, keepdims=True) + EPS)
    return (y * g.astype(jnp.float32)).astype(x.dtype)


def layer_norm(x, g, b):
    xf = x.astype(jnp.float32)
    mu = jnp.mean(xf, axis=-1, keepdims=True)
    xc = xf - mu
    var = jnp.mean(xc * xc, axis=-1, keepdims=True)
    return (xc * lax.rsqrt(var + EPS) * g.astype(jnp.float32) + b.astype(jnp.float32)).astype(x.dtype)


def ada_modulation(c, w_ada, b_ada):
    m = jax.nn.silu(c) @ w_ada + b_ada
    return jnp.split(m[:, None, :], 6, axis=-1)


def modulate(x, g, shift, scale):
    return rms_norm(x, g) * (1 + scale) + shift


def pool_mixer(h, buf, pos0, w_pool, pool_scale):
    B, L, D = h.shape
    xx = jnp.concatenate([buf.astype(h.dtype), h], axis=1)
    cs = jnp.cumsum(xx.astype(jnp.float32), axis=1)
    cs = jnp.pad(cs, ((0, 0), (1, 0), (0, 0)))
    pos = pos0 + jnp.arange(L)
    hi = cs[:, POOL_BUF + 1:POOL_BUF + 1 + L]
    means = []
    for gi, w in enumerate(POOL_WINDOWS):
        sl = slice(gi * POOL_GROUP, (gi + 1) * POOL_GROUP)
        lo = cs[:, POOL_BUF + 1 - w:POOL_BUF + 1 - w + L, sl]
        cnt = jnp.minimum(w, pos + 1).astype(jnp.float32)[None, :, None]
        means.append((hi[:, :, sl] - lo) / cnt)
    mean = jnp.concatenate(means, axis=-1)
    d = (mean - h.astype(jnp.float32)).astype(h.dtype)
    d = d.reshape(B, L, N_POOL_GROUPS, POOL_GROUP)
    y = jnp.einsum('blgc,gcd->blgd', d, w_pool).reshape(B, L, D)
    return y * pool_scale, xx[:, -POOL_BUF:]


def sgu_mixer(h, w_in, b_in, ln_g, ln_b, w_sp, b_sp, w_out):
    B, L, _ = h.shape
    z = jax.nn.gelu(h @ w_in + b_in, approximate=False)
    u, v = jnp.split(z, 2, axis=-1)
    v = layer_norm(v, ln_g, ln_b)
    n_chunks = -(-L // CHUNK)
    pad = n_chunks * CHUNK - L
    vc = jnp.pad(v, ((0, 0), (0, pad), (0, 0))).reshape(B, n_chunks, CHUNK, N_SGU_GROUPS, SGU_GROUP)
    mask = jnp.tril(jnp.ones((CHUNK, CHUNK), dtype=bool))
    w_eff = jnp.where(mask[None], w_sp, 0).astype(vc.dtype)
    mixed = jnp.einsum('gts,bnsgc->bntgc', w_eff, vc) + b_sp.T[None, None, :, :, None]
    mixed = mixed.reshape(B, n_chunks * CHUNK, D_SGU)[:, :L]
    y = (u * mixed) @ w_out
    last_start = ((L - 1) // CHUNK) * CHUNK
    return y, v[:, last_start:]


def channel_mlp(h, w1, w2):
    a = jax.nn.relu(h @ w1)
    return (a * a) @ w2


def trunk(x, c, pool_buf, pos0, norm_g, w_ada, b_ada, w_pool, pool_scale,
          sgu_w_in, sgu_b_in, sgu_ln_g, sgu_ln_b, sgu_w_sp, sgu_b_sp, sgu_w_out,
          mlp_w1, mlp_w2, final_g):
    new_pool, new_v = [], []
    for i in range(DEPTH):
        sh1, sc1, g1, sh2, sc2, g2 = ada_modulation(c, w_ada[i], b_ada[i])
        h = modulate(x, norm_g[i, 0], sh1, sc1)
        j = i // N_MIXERS
        if i % N_MIXERS == 0:
            buf = pool_buf[j] if pool_buf is not None else jnp.zeros((x.shape[0], POOL_BUF, x.shape[2]), x.dtype)
            y, nb = pool_mixer(h, buf, pos0, w_pool[j], pool_scale[j])
            new_pool.append(nb)
        else:
            y, vr = sgu_mixer(h, sgu_w_in[j], sgu_b_in[j], sgu_ln_g[j], sgu_ln_b[j],
                              sgu_w_sp[j], sgu_b_sp[j], sgu_w_out[j])
            new_v.append(vr)
        x = x + g1 * y
        h = modulate(x, norm_g[i, 1], sh2, sc2)
        x = x + g2 * channel_mlp(h, mlp_w1[i], mlp_w2[i])
    return rms_norm(x, final_g), jnp.stack(new_pool), jnp.stack(new_v)


def setup_inputs(seed: int = 0) -> dict:
    key = jax.random.key(seed)
    ks = jax.random.split(key, 24)
    f32 = jnp.float32
    nrm = lambda k, s, scale: jax.random.normal(k, s, f32) * scale
    return {
        "x_prompt": nrm(ks[0], (BATCH, SEQ, D_MODEL), 1.0),
        "x_sample": nrm(ks[1], (DEC_BATCH, DEC_SEQ, D_MODEL), 1.0),
        "c_prompt": nrm(ks[2], (BATCH, D_MODEL), 1.0),
        "c_sample": nrm(ks[3], (DEC_BATCH, D_MODEL), 1.0),
        "state_pool": nrm(ks[4], (N_POOL_LAYERS, DEC_BATCH, POOL_BUF, D_MODEL), 1.0),
        "norm_g": 1.0 + nrm(ks[5], (DEPTH, 2, D_MODEL), 0.05),
        "w_ada": nrm(ks[6], (DEPTH, D_MODEL, 6 * D_MODEL), 0.5 * D_MODEL ** -0.5),
        "b_ada": nrm(ks[7], (DEPTH, 6 * D_MODEL), 0.01),
        "w_pool": nrm(ks[8], (N_POOL_LAYERS, N_POOL_GROUPS, POOL_GROUP, POOL_GROUP), POOL_GROUP ** -0.5),
        "pool_scale": 1.0 + nrm(ks[9], (N_POOL_LAYERS, D_MODEL), 0.05),
        "sgu_w_in": nrm(ks[10], (N_SGU_LAYERS, D_MODEL, 2 * D_SGU), D_MODEL ** -0.5),
        "sgu_b_in": nrm(ks[11], (N_SGU_LAYERS, 2 * D_SGU), 0.01),
        "sgu_ln_g": 1.0 + nrm(ks[12], (N_SGU_LAYERS, D_SGU), 0.05),
        "sgu_ln_b": nrm(ks[13], (N_SGU_LAYERS, D_SGU), 0.01),
        "sgu_w_sp": nrm(ks[14], (N_SGU_LAYERS, N_SGU_GROUPS, CHUNK, CHUNK), CHUNK ** -0.5),
        "sgu_b_sp": 1.0 + nrm(ks[15], (N_SGU_LAYERS, N_SGU_GROUPS, CHUNK), 0.05),
        "sgu_w_out": nrm(ks[16], (N_SGU_LAYERS, D_SGU, D_MODEL), D_SGU ** -0.5),
        "mlp_w1": nrm(ks[17], (DEPTH, D_MODEL, D_FF), D_MODEL ** -0.5),
        "mlp_w2": nrm(ks[18], (DEPTH, D_FF, D_MODEL), D_FF ** -0.5),
        "final_g": 1.0 + nrm(ks[19], (D_MODEL,), 0.05),
    }


def reference(x_prompt, x_sample, c_prompt, c_sample, state_pool, norm_g, w_ada, b_ada,
              w_pool, pool_scale, sgu_w_in, sgu_b_in, sgu_ln_g, sgu_ln_b, sgu_w_sp,
              sgu_b_sp, sgu_w_out, mlp_w1, mlp_w2, final_g):
    y_prompt, new_pool_prompt, new_sgu_v_prompt = trunk(
        x_prompt, c_prompt, None, 0, norm_g, w_ada, b_ada, w_pool, pool_scale,
        sgu_w_in, sgu_b_in, sgu_ln_g, sgu_ln_b, sgu_w_sp, sgu_b_sp, sgu_w_out,
        mlp_w1, mlp_w2, final_g)
    y_sample, new_pool_sample, new_sgu_v_sample = trunk(
        x_sample, c_sample, state_pool, PAST_LEN, norm_g, w_ada, b_ada, w_pool, pool_scale,
        sgu_w_in, sgu_b_in, sgu_ln_g, sgu_ln_b, sgu_w_sp, sgu_b_sp, sgu_w_out,
        mlp_w1, mlp_w2, final_g)
    return (y_prompt, y_sample, new_pool_prompt, new_pool_sample, new_sgu_v_prompt, new_sgu_v_sample)
```

```python
import numpy as np
import concourse.bass as bass
import concourse.mybir as mybir
from concourse.bass_utils import run_bass_kernel_spmd

F32 = mybir.dt.float32
BF16 = mybir.dt.bfloat16
ALU = mybir.AluOpType
AF = mybir.ActivationFunctionType

D = 1024
NDC = 8
TP = 2048
TS = 128
T = TP + TS
NSEQ = 17
DFF = 4096
EPS = 1e-6
HW = 2432
SOFF = 2064
NS = 6
GROUPS = [(0, 512), (512, 512), (1024, 512), (1536, 512), (2048, 128)]

ENGS = ("pe", "act", "dve", "pool", "sp")


class Op:
    __slots__ = ("eng", "emit", "deps", "signal", "sigval", "stream", "idx")

    def __init__(self, eng, emit, stream):
        self.eng = eng
        self.emit = emit
        self.deps = []
        self.signal = False
        self.sigval = None
        self.stream = stream


class Sched:
    def __init__(self, nc):
        self.nc = nc
        self.ops = {e: [] for e in ENGS}
        self.rec = {}
        self.nops = 0

    def add(self, eng, emit, reads=(), writes=(), stream=None):
        op = Op(eng, emit, stream)
        op.idx = self.nops
        self.nops += 1
        deps = {}

        def add_dep(d):
            if d.stream is None and d.eng == "pe" and eng == "pe":
                return
            k = ("dma", id(d)) if d.stream is not None else ("eng", d.eng)
            cur = deps.get(k)
            if cur is None or d.idx > cur.idx:
                deps[k] = d

        for (key, lo, hi) in reads:
            for r in self.rec.get(key, ()):
                if r[2] == "w" and r[0] < hi and lo < r[1]:
                    add_dep(r[3])
        for (key, lo, hi) in writes:
            for r in self.rec.get(key, ()):
                if r[0] < hi and lo < r[1]:
                    add_dep(r[3])
        for (key, lo, hi) in writes:
            lst = self.rec.setdefault(key, [])
            lst[:] = [r for r in lst if not (lo <= r[0] and r[1] <= hi)]
            lst.append([lo, hi, "w", op])
        for (key, lo, hi) in reads:
            lst = self.rec.setdefault(key, [])
            if stream is None:
                lst[:] = [r for r in lst if not (r[2] == "r" and r[3].eng == eng and r[3].stream is None
                                                 and lo <= r[0] and r[1] <= hi)]
            lst.append([lo, hi, "r", op])
        op.deps = list(deps.values())
        for d in op.deps:
            d.signal = True
        self.ops[eng].append(op)
        return op

    def emit_all(self):
        nc = self.nc
        esem = {e: nc.alloc_semaphore("sem_" + e) for e in ENGS}
        ssem, scount = {}, {}
        ecount = {e: 0 for e in ENGS}
        for e in ENGS:
            for op in self.ops[e]:
                if op.stream is not None:
                    if op.stream not in ssem:
                        ssem[op.stream] = nc.alloc_semaphore("dma_" + op.stream)
                        scount[op.stream] = 0
                    scount[op.stream] += 16
                    op.sigval = (ssem[op.stream], scount[op.stream])
                elif op.signal:
                    ecount[e] += 1
                    op.sigval = (esem[e], ecount[e])
        self.stats = {e: (len(self.ops[e]), ecount[e]) for e in ENGS}
        handles = {"pe": "tensor", "act": "scalar", "dve": "vector", "pool": "gpsimd", "sp": "sync"}
        final = dict((s, (ssem[s], scount[s])) for s in ssem)
        with nc.Block() as block:
            for e in ENGS:
                def body(eng, ops=self.ops[e], e=e):
                    waited = {}
                    for op in ops:
                        need = {}
                        for d in op.deps:
                            sem, val = d.sigval
                            k = id(sem)
                            if k not in need or need[k][1] < val:
                                need[k] = (sem, val)
                        for k, (sem, val) in need.items():
                            if waited.get(k, 0) >= val:
                                continue
                            eng.wait_ge(sem, val)
                            waited[k] = val
                        ins = op.emit(eng)
                        if op.stream is not None:
                            ins.then_inc(op.sigval[0], 16)
                        elif op.signal:
                            ins.then_inc(op.sigval[0], 1)
                    if e == "sp":
                        for s, (sem, val) in final.items():
                            if waited.get(id(sem), 0) < val:
                                eng.wait_ge(sem, val)
                        for e2 in ENGS:
                            if e2 != e and ecount[e2] > 0:
                                eng.wait_ge(esem[e2], ecount[e2])

                getattr(block, handles[e])(body)


def build_program():
    nc = bass.Bass("TRN2", target_bir_lowering=False)
    S = Sched(nc)

    def din(name, shape):
        return nc.dram_tensor(name, list(shape), F32, kind="ExternalInput")

    def dout(name, shape):
        return nc.dram_tensor(name, list(shape), F32, kind="ExternalOutput")

    d_xT = din("xT", [D, T])
    d_cT = din("cT", [D, NSEQ])
    d_spT = din("spT", [D, 368])
    d_spt = din("sp_tail", [D, 112])
    d_vecs = din("vecs", [128, 152])
    d_wada = din("w_ada", [2, D, 6 * D])
    d_wpool = din("w_pool", [4, 256, 256])
    d_win = din("sgu_w_in", [D, 2 * D])
    d_wout = din("sgu_w_out", [D, D])
    d_w1 = din("mlp_w1", [2, D, DFF])
    d_w2 = din("mlp_w2", [2, DFF, D])
    d_wsp = din("wsp_pack", [128, 4, 2, 128])
    d_msk = din("msk_pack", [128, 2, 128])
    d_bsp = din("bsp_pack", [1, 4, 2, 128])
    d_binv = din("binv", [1, D])
    d_lng = din("ln_g", [1, D])
    d_lnb = din("ln_b", [1, D])
    d_rvec = din("rvec", [128, 4, 16])
    o_yT = dout("yT", [D, T])
    o_npp = dout("nppT", [D, 15])
    o_npsa = dout("npsTa", [D, 112])
    o_npsb = dout("npsTb", [D, 128])
    o_vP = dout("vP", [128, D])
    o_vS = dout("vS", [128, D])

    def sb(name, shape, dt):
        return nc.alloc_sbuf_tensor(name, list(shape), dt)

    xT = sb("xT_sb", [128, NDC, T], F32)
    hT = sb("hT_sb", [128, NDC, HW], BF16)
    ring = [sb("ring%d" % i, [128, 4096], BF16) for i in range(NS)]
    aT = [sb("aT%d" % i, [128, 4, 512], BF16) for i in range(2)]
    scr = sb("scr", [128, 4096], F32)
    rstd = [sb("rstd%d" % i, [128, 512], F32) for i in range(2)]
    sq = [sb("sq%d" % i, [128, 512], BF16) for i in range(7)]
    modT = sb("modT", [128, 2, 6, NDC, NSEQ], F32)
    DER = {0: 1, 1: 4, 2: 2, 3: 5}
    vecs = sb("vecs_sb", [128, 152], F32)
    ones = sb("ones", [128, 128], BF16)
    cT = sb("cT_sb", [128, NDC, NSEQ], F32)
    siluT = sb("siluT", [128, NDC, NSEQ], BF16)
    wsp = sb("wsp", [128, 4, 2, 128], BF16)
    btile = sb("btile", [128, 4, 128], F32)
    binv = sb("binv_sb", [1, D], BF16)
    lnst = sb("lnst", [128, 16], F32)
    mhalf = sb("mhalf", [128, 1], F32)
    epsd = sb("epsd", [128, 1], F32)
    aux4k = sb("aux4k", [128, 2048], BF16)
    rvec = sb("rvec_sb", [128, 4, 16], F32)
    print("sbuf bytes remaining:", nc.sbuf_bytes_remaining)

    psA = [nc.alloc_psum_tensor("psA%d" % i, [128, 512], F32) for i in range(3)]
    psB = [nc.alloc_psum_tensor("psB%d" % i, [128, 512], F32) for i in range(3)]
    psS = nc.alloc_psum_tensor("psS", [128, 512], F32)
    psM = nc.alloc_psum_tensor("psM", [128, 512], F32)
    rot = {"A": 0, "A4m": 0, "B": 0, "B4": 0, "M": 0, "sq": 0, "rstd": 0}

    def nxt(kind, n):
        v = rot[kind]
        rot[kind] = (v + 1) % n
        return v

    scr_bf = scr.ap().bitcast(BF16)

    def SCR(lo_b, hi_b):
        return ("scr", lo_b, hi_b)

    S.add("sp", lambda e: e.dma_start(out=vecs.ap(), in_=d_vecs.ap()), writes=[("vecs", 0, 152)], stream="s_vecs")
    S.add("sp", lambda e: e.dma_start(out=cT.ap(), in_=d_cT.ap().rearrange("(dc p) b -> p dc b", p=128)),
          writes=[("cT", 0, 1)], stream="s_cT")
    S.add("sp", lambda e: e.dma_start(out=rvec.ap(), in_=d_rvec.ap()), writes=[("rvec", 0, 1)], stream="s_rvec")
    S.add("dve", lambda e: e.memset(ones.ap(), 1.0), writes=[("ones", 0, 1)])
    S.add("dve", lambda e: e.tensor_scalar(out=vecs.ap()[:, 48:56], in0=vecs.ap()[:, 48:56], scalar1=float(np.sqrt(D)), scalar2=None,
                                           op0=ALU.mult), reads=[("vecs", 0, 152)], writes=[("vecs", 0, 152)])
    S.add("dve", lambda e: e.tensor_scalar(out=vecs.ap()[:, 0:32], in0=vecs.ap()[:, 0:32], scalar1=float(np.sqrt(D)), scalar2=None,
                                           op0=ALU.mult), reads=[("vecs", 0, 152)], writes=[("vecs", 0, 152)])
    S.add("dve", lambda e: e.memset(mhalf.ap(), -0.5), writes=[("mhalf", 0, 1)])
    S.add("dve", lambda e: e.memset(epsd.ap(), float(D * EPS)), writes=[("epsd", 0, 1)])
    for dc in range(NDC):
        S.add("dve", lambda e, dc=dc: e.memset(hT.ap()[:, dc, 0:16], 0.0), writes=[(("hT", dc), 0, 16)])
    S.add("act", lambda e: e.activation(out=siluT.ap(), in_=cT.ap(), func=AF.Silu),
          reads=[("cT", 0, 1)], writes=[("siluT", 0, 1)])

    pieces = []

    def colpiece(src2d, c0, ncols=512):
        return src2d[:, c0:c0 + ncols].rearrange("(dc p) j -> p dc j", p=128), (NDC, ncols)

    def rowpiece(src2d, r0):
        return src2d[r0:r0 + 512, :].rearrange("(fc p) j -> p fc j", p=128), (4, 1024)

    for a in range(12):
        pieces.append(("ada", 0, a) + colpiece(d_wada.ap()[0], a * 512))
    for a in range(12):
        pieces.append(("ada", 1, a) + colpiece(d_wada.ap()[1], a * 512))
    for blk in range(8):
        pieces.append(("w1", 0, blk) + colpiece(d_w1.ap()[0], blk * 512))
        pieces.append(("w2", 0, blk) + rowpiece(d_w2.ap()[0], blk * 512))
    for h in range(2):
        pieces.append(("inu", 1, h) + colpiece(d_win.ap(), h * 512))
    for h in range(2):
        pieces.append(("inv", 1, h) + colpiece(d_win.ap(), D + h * 512))
    for h in range(2):
        pieces.append(("out", 1, h) + colpiece(d_wout.ap(), h * 512))
    for blk in range(8):
        pieces.append(("w1", 1, blk) + colpiece(d_w1.ap()[1], blk * 512))
        pieces.append(("w2", 1, blk) + rowpiece(d_w2.ap()[1], blk * 512))
    pidx = {(p[0], p[1], p[2]): i for i, p in enumerate(pieces)}
    state = {"issued": 0}

    def pview(i):
        assert i < state["issued"], "ring piece %d consumed before its DMA was issued" % i
        a, b = pieces[i][4]
        return ring[i % NS].ap()[:, 0:a * b].rearrange("p (a b) -> p a b", b=b)

    def preg(i):
        return ("ring", i % NS), 0, 4096

    consumed = set()

    def issue_ready(limit=None, extra_reads=()):
        while state["issued"] < (len(pieces) if limit is None else limit):
            i = state["issued"]
            if i >= NS and (i - NS) not in consumed:
                return
            state["issued"] += 1
            src = pieces[i][3]
            dst = pview(i)
            S.add("pool", lambda e, dst=dst, src=src: e.dma_start(out=dst, in_=src), writes=[preg(i)], reads=list(extra_reads),
                  stream="ring%d" % (i % NS))

    def done_piece(i):
        consumed.add(i)
        issue_ready()

    issue_ready(limit=4)
    for g, (t0, n) in enumerate(GROUPS):
        if g == 2:
            issue_ready(limit=NS, extra_reads=[(("xT", 0), 512, 1024)])
        S.add("sp", lambda e, t0=t0, n=n: e.dma_start(out=xT.ap()[:, :, t0:t0 + n],
                                                      in_=d_xT.ap()[:, t0:t0 + n].rearrange("(dc p) t -> p dc t", p=128)),
              writes=[(("xT", dc), t0, t0 + n) for dc in range(NDC)], stream="s_x%d" % g,
              reads=([(("ring", 3), 0, 4096)] if g >= 2 else []))
    issue_ready()
    S.add("pool", lambda e: e.dma_start(out=aux4k.ap().rearrange("p (a b) -> p a b", b=256),
                                        in_=d_wpool.ap().rearrange("g (cc p) j -> p (g cc) j", p=128)),
          writes=[("aux4k", 0, 1)], stream="s_wpool")
    S.add("pool", lambda e: e.dma_start(out=hT.ap()[:, :, SOFF:SOFF + 368], in_=d_spT.ap().rearrange("(dc p) j -> p dc j", p=128)),
          writes=[(("hT", dc), SOFF, SOFF + 368) for dc in range(NDC)], stream="s_sp")
    S.add("sp", lambda e: e.dma_start(out=o_npsa.ap(), in_=d_spt.ap()), stream="o_nps0")

    def vcol(c):
        return vecs.ap()[:, c:c + 1]

    def ada_piece(l, a):
        pi = pidx[("ada", l, a)]
        wv = pview(pi)
        bank, bkey = psM, "psM"
        for jj in range(4):
            pso = bank.ap()[:, jj * 32: jj * 32 + NSEQ]
            for dc in range(NDC):
                S.add("pe", lambda e, pso=pso, wv=wv, dc=dc, jj=jj: e.matmul(
                    pso, wv[:, dc, jj * 128:(jj + 1) * 128], siluT.ap()[:, dc, :], start=(dc == 0), stop=(dc == NDC - 1)),
                    reads=[preg(pi), ("siluT", 0, 1)], writes=[(bkey, 0, 512)])
        for jj in range(4):
            jc = a * 4 + jj
            pso = bank.ap()[:, jj * 32: jj * 32 + NSEQ]
            v, dcj = jc // 8, jc % 8
            S.add("act", lambda e, pso=pso, l=l, v=v, dcj=dcj, jc=jc: e.activation(
                out=modT.ap()[:, l, v, dcj, :], in_=pso, func=AF.Identity, bias=vcol(56 + l * 48 + jc), scale=1.0),
                reads=[(bkey, 0, 512), ("vecs", 0, 152)],
                writes=[(("modT", l, v), dcj, dcj + 1)])
        done_piece(pi)

    def ada_derive_A(l, k):
        vsc, nrm = [(1, 0), (4, 1)][k]
        for dc in range(NDC):
            S.add("dve", lambda e, dc=dc: e.tensor_scalar(
                out=modT.ap()[:, l, DER[k], dc, :], in0=modT.ap()[:, l, vsc, dc, :], scalar1=1.0,
                scalar2=vcol((l * 2 + nrm) * 8 + dc), op0=ALU.add, op1=ALU.mult),
                reads=[(("modT", l, vsc), dc, dc + 1), ("vecs", 0, 152)], writes=[(("modT", l, DER[k]), dc, dc + 1)])

    def ada_derive_G1(l):
        for dc in range(NDC):
            if l == 0:
                w = float(2 ** (dc // 2 + 1))
                S.add("dve", lambda e, dc=dc, w=w: e.tensor_scalar(
                    out=modT.ap()[:, 0, DER[2], dc, :], in0=modT.ap()[:, 0, 2, dc, :], scalar1=vcol(32 + dc), scalar2=1.0 / w,
                    op0=ALU.mult, op1=ALU.mult),
                    reads=[(("modT", 0, 2), dc, dc + 1), ("vecs", 0, 152)], writes=[(("modT", 0, DER[2]), dc, dc + 1)])

    def ada_derive_G2(l):
        pass

    def bc_seq(t_ap5, l, k, dc):
        base = t_ap5[:, l, k, dc, 1:2]
        return bass.AP(base.tensor, base.offset, [list(base.ap[0]), [1, 16], [0, 8]])

    def hcols(g):
        t0, n = GROUPS[g]
        if g < 4:
            return 16 + t0, n
        return SOFF, n

    norm_state = {}

    def norm_group(l, k, g, pool_layout=False, final=False, part="all", shift_eng="dve"):
        vA = k
        vsh = 0 if k == 0 else 3
        for (t0, n) in [GROUPS[g]]:
          if part in ("all", "stats", "stats_a"):
            for dc in range(NDC):
                si = nxt("sq", 7)
                S.add("act", lambda e, si=si, dc=dc, t0=t0, n=n: e.activation(
                    out=sq[si].ap()[:, 0:n], in_=xT.ap()[:, dc, t0:t0 + n], func=AF.Square),
                    reads=[(("xT", dc), t0, t0 + n)], writes=[(("sq", si), 0, 512)])
                S.add("pe", lambda e, si=si, dc=dc, n=n: e.matmul(
                    psS.ap()[:, 0:n], ones.ap(), sq[si].ap()[:, 0:n], start=(dc == 0), stop=(dc == NDC - 1)),
                    reads=[(("sq", si), 0, 512), ("ones", 0, 1)], writes=[("psS", 0, 512)])
          if part in ("all", "stats", "stats_b"):
            ri = nxt("rstd", 2)
            R = rstd[ri].ap()[:, 0:n]
            S.add("act", lambda e, R=R, n=n: e.activation(out=R, in_=psS.ap()[:, 0:n], func=AF.Ln, bias=epsd.ap()[:, 0:1], scale=1.0),
                  reads=[("psS", 0, 512), ("epsd", 0, 1)], writes=[(("rstd", ri), 0, 512)])
            S.add("act", lambda e, R=R: e.activation(out=R, in_=R, func=AF.Exp, scale=-0.5),
                  reads=[(("rstd", ri), 0, 512)], writes=[(("rstd", ri), 0, 512)])
            norm_state[(l, k, g, final)] = ri
          if part in ("all", "mod"):
            ri = norm_state[(l, k, g, final)]
            R = rstd[ri].ap()[:, 0:n]
            for dc in range(NDC):
                xs = xT.ap()[:, dc, t0:t0 + n]
                xr = [(("xT", dc), t0, t0 + n)]
                if final:
                    S.add("dve", lambda e, xs=xs, R=R, dc=dc: e.scalar_tensor_tensor(
                        out=xs, in0=xs, scalar=vcol(48 + dc), in1=R, op0=ALU.mult, op1=ALU.mult),
                        reads=xr + [(("rstd", ri), 0, 512), ("vecs", 0, 152)], writes=xr)
                    continue
                if g < 4:
                    c0 = 16 + t0
                    hs = hT.ap()[:, dc, c0:c0 + n]
                    hreg = [(("hT", dc), c0, c0 + n)]
                    S.add("dve", lambda e, hs=hs, xs=xs, R=R, dc=dc: e.scalar_tensor_tensor(
                        out=hs, in0=xs, scalar=modT.ap()[:, l, DER[vA], dc, 0:1], in1=R, op0=ALU.mult, op1=ALU.mult),
                        reads=xr + [(("rstd", ri), 0, 512), (("modT", l, DER[vA]), dc, dc + 1)], writes=hreg)
                    if shift_eng == "act":
                        S.add("act", lambda e, hs=hs, dc=dc: e.activation(
                            out=hs, in_=hs, func=AF.Identity, bias=modT.ap()[:, l, vsh, dc, 0:1], scale=1.0),
                            reads=hreg + [(("modT", l, vsh), dc, dc + 1)], writes=hreg)
                    else:
                        S.add("dve", lambda e, hs=hs, dc=dc: e.tensor_scalar(
                            out=hs, in0=hs, scalar1=modT.ap()[:, l, vsh, dc, 0:1], scalar2=None, op0=ALU.add),
                            reads=hreg + [(("modT", l, vsh), dc, dc + 1)], writes=hreg)
                else:
                    tmp = rstd[ri].ap()[:, 128:256].rearrange("p (b t) -> p b t", t=8)
                    treg = [(("rstd", ri), 0, 512)]
                    x3 = xs.rearrange("p (b t) -> p b t", t=8)
                    R3 = R.rearrange("p (b t) -> p b t", t=8)
                    S.add("dve", lambda e, tmp=tmp, x3=x3, dc=dc: e.tensor_tensor(
                        out=tmp, in0=x3, in1=bc_seq(modT.ap(), l, DER[vA], dc), op=ALU.mult),
                        reads=xr + [(("modT", l, DER[vA]), dc, dc + 1)], writes=treg)
                    S.add("dve", lambda e, tmp=tmp, R3=R3: e.tensor_tensor(out=tmp, in0=tmp, in1=R3, op=ALU.mult),
                          reads=treg + [(("rstd", ri), 0, 512)], writes=treg)
                    if pool_layout:
                        hs3 = hT.ap()[:, dc, SOFF:SOFF + 368].rearrange("p (b j) -> p b j", j=23)[:, :, 15:23]
                        hreg = [(("hT", dc), SOFF, SOFF + 368)]
                        h32 = scr.ap()[:, dc * 128:(dc + 1) * 128].rearrange("p (b t) -> p b t", t=8)
                        S.add("dve", lambda e, tmp=tmp, h32=h32, dc=dc: e.tensor_tensor(
                            out=h32, in0=tmp, in1=bc_seq(modT.ap(), l, vsh, dc), op=ALU.add),
                            reads=treg + [(("modT", l, vsh), dc, dc + 1)], writes=[SCR(dc * 512, (dc + 1) * 512)])
                        S.add("dve", lambda e, hs3=hs3, h32=h32: e.tensor_copy(out=hs3, in_=h32),
                              reads=[SCR(dc * 512, (dc + 1) * 512)], writes=hreg)
                    else:
                        hs3 = hT.ap()[:, dc, SOFF:SOFF + 128].rearrange("p (b t) -> p b t", t=8)
                        hreg = [(("hT", dc), SOFF, SOFF + 128)]
                        S.add("dve", lambda e, tmp=tmp, hs3=hs3, dc=dc: e.tensor_tensor(
                            out=hs3, in0=tmp, in1=bc_seq(modT.ap(), l, vsh, dc), op=ALU.add),
                            reads=treg + [(("modT", l, vsh), dc, dc + 1)], writes=hreg)

    def residual(l, kG, ps_ap, dout_c, g):
        t0, n = GROUPS[g]
        xs = xT.ap()[:, dout_c, t0:t0 + n]
        xr = [(("xT", dout_c), t0, t0 + n)]
        return xs, xr

    def add_residual(l, kG, bank_key, ps_ap, dout_c, g):
        t0, n = GROUPS[g]
        xs = xT.ap()[:, dout_c, t0:t0 + n]
        xr = [(("xT", dout_c), t0, t0 + n)]
        if g < 4:
            S.add("dve", lambda e: e.scalar_tensor_tensor(
                out=xs, in0=ps_ap, scalar=modT.ap()[:, l, DER[kG], dout_c, 0:1], in1=xs, op0=ALU.mult, op1=ALU.add),
                reads=xr + [bank_key, (("modT", l, DER[kG]), dout_c, dout_c + 1)], writes=xr)
        else:
            tmp = scr.ap()[:, 3840:3968].rearrange("p (b t) -> p b t", t=8)
            treg = [SCR(15360, 15872)]
            S.add("dve", lambda e: e.tensor_tensor(out=tmp, in0=ps_ap.rearrange("p (b t) -> p b t", t=8),
                                                   in1=bc_seq(modT.ap(), l, DER[kG], dout_c), op=ALU.mult),
                  reads=[bank_key, (("modT", l, DER[kG]), dout_c, dout_c + 1)], writes=treg)
            S.add("dve", lambda e: e.tensor_tensor(out=xs.rearrange("p (b t) -> p b t", t=8),
                                                   in0=xs.rearrange("p (b t) -> p b t", t=8), in1=tmp, op=ALU.add),
                  reads=xr + treg, writes=xr)

    def mlp(l, after_blk=None, after_last=None, lazy=(), flush=True):
        steps = [(blk, g) for blk in range(8) for g in range(5)]
        pending = []
        lazy = list(lazy)

        def mm1(si):
            blk, g = steps[si]
            c0, n = hcols(g)
            p1 = pidx[("w1", l, blk)]
            wv = pview(p1)
            ab = si % 2
            for fc in range(4):
                if 1 <= blk <= 6:
                    bank, bkey = [(psA[0], ("psA", 0)), (psA[1], ("psA", 1)), (psA[2], ("psA", 2)), (psS, "psS")][nxt("A4m", 4)]
                else:
                    bi = nxt("A", 3)
                    bank, bkey = psA[bi], ("psA", bi)
                pso = bank.ap()[:, 0:n]
                for dc in range(NDC):
                    S.add("pe", lambda e, pso=pso, wv=wv, dc=dc, fc=fc, c0=c0, n=n: e.matmul(
                        pso, wv[:, dc, fc * 128:(fc + 1) * 128], hT.ap()[:, dc, c0:c0 + n],
                        start=(dc == 0), stop=(dc == NDC - 1)),
                        reads=[preg(p1), (("hT", dc), c0, c0 + n)], writes=[(bkey, 0, 512)])
                av = aT[ab].ap()[:, fc, 0:n]
                areg = [(("aT", ab), fc * 512, (fc + 1) * 512)]
                S.add("act", lambda e, av=av, pso=pso: e.activation(out=av, in_=pso, func=AF.Relu),
                      reads=[(bkey, 0, 512)], writes=areg)
                S.add("act", lambda e, av=av: e.activation(out=av, in_=av, func=AF.Square),
                      reads=areg, writes=areg)
            if g == 4:
                done_piece(p1)
                if after_blk is not None:
                    after_blk(blk)

        def mm2(si):
            blk, g = steps[si]
            t0, n = GROUPS[g]
            p2 = pidx[("w2", l, blk)]
            wv = pview(p2)
            ab = si % 2
            for dc in range(NDC):
                bank, bkey = [(psB[0], ("psB", 0)), (psB[1], ("psB", 1)), (psB[2], ("psB", 2)), (psM, "psM")][nxt("B4", 4)]
                pso = bank.ap()[:, 0:n]
                for fc in range(4):
                    S.add("pe", lambda e, pso=pso, wv=wv, dc=dc, fc=fc, n=n, ab=ab: e.matmul(
                        pso, wv[:, fc, dc * 128:(dc + 1) * 128], aT[ab].ap()[:, fc, 0:n],
                        start=(fc == 0), stop=(fc == 3)),
                        reads=[preg(p2), (("aT", ab), fc * 512, (fc + 1) * 512)], writes=[(bkey, 0, 512)])
                add_residual(l, 3, (bkey, 0, 512), pso, dc, g)
            if g == 4:
                done_piece(p2)
            if blk == 7 and after_last is not None:
                pending.append((g + 1, lambda g=g: after_last(g, "stats")))
                pending.append((g + 2, lambda g=g: after_last(g, "mod")))
                pending.sort(key=lambda t: t[0])

        for i in range(len(steps) + 1):
            if i < len(steps):
                mm1(i)
                if lazy and i < 3:
                    lazy.pop(0)()
                    while i == 2 and lazy:
                        lazy.pop(0)()
            if i >= 1:
                bprev, gprev = steps[i - 1]
                if bprev == 7:
                    while pending and pending[0][0] <= gprev:
                        pending.pop(0)[1]()
                mm2(i - 1)
        rest_ = [f_ for (_, f_) in pending]
        if flush:
            for f_ in rest_:
                f_()
            return []
        return rest_

    def pool_mixer(after_group=None, mid_group=None):
        l = 0
        wv = aux4k.ap().rearrange("p (a b) -> p a b", b=256)
        Pall = [scr_bf[:, 2048 + i * 528: 2048 + (i + 1) * 528] for i in range(4)]
        Pregall = [SCR(4096 + i * 1056, 4096 + (i + 1) * 1056) for i in range(4)]
        Wn = scr_bf[:, 4160:6208].rearrange("p (a b) -> p a b", b=256)
        for k_ in range(8):
            S.add("act", lambda e, k_=k_: e.activation(out=Wn[:, k_, :], in_=wv[:, k_, :], func=AF.Identity, scale=-float(2 ** (k_ // 2 + 1))),
                  reads=[("aux4k", 0, 1)], writes=[SCR(8320 + k_ * 512, 8320 + (k_ + 1) * 512)])
        dT = [aT[0].ap()[:, i, :] for i in range(4)] + [aT[1].ap()[:, i, :] for i in range(4)]
        dreg = [[(("aT", 0), i * 512, (i + 1) * 512)] for i in range(4)] + [[(("aT", 1), i * 512, (i + 1) * 512)] for i in range(4)]
        for g, (t0, n) in enumerate(GROUPS):
            for dc in range(NDC):
                gi = dc // 2
                w = 2 ** (gi + 1)
                hk = ("hT", dc)
                en = "dve"
                P = Pall[0:2]
                Preg = Pregall[0:2]
                if g < 4:
                    c0 = 16 + t0

                    def hv(sh, lo=0, hi=n, dc=dc, c0=c0):
                        return hT.ap()[:, dc, c0 + lo - sh:c0 + hi - sh]
                    hreg = [(hk, c0 - 16, c0 + n)]
                    dv = dT[dc][:, 0:n]
                    if gi == 0:
                        S.add("dve", lambda e, dv=dv, hv=hv: e.tensor_tensor(out=dv, in0=hv(1), in1=hv(0), op=ALU.add),
                              reads=hreg, writes=dreg[dc])
                    else:
                        ext = n + 16
                        if w == 4:
                            pass
                        S.add(en, lambda e, dc=dc, c0=c0, ext=ext, P=P: e.tensor_tensor(
                            out=P[0][:, 1:ext], in0=hT.ap()[:, dc, c0 - 15:c0 + ext - 16],
                            in1=hT.ap()[:, dc, c0 - 16:c0 + ext - 17], op=ALU.add),
                            reads=hreg, writes=[Preg[0]])
                        cur, lev, lo = 0, 2, 1
                        while lev * 2 < w:
                            nlo = lo + lev
                            S.add(en, lambda e, cur=cur, lev=lev, nlo=nlo, ext=ext, P=P: e.tensor_tensor(
                                out=P[1 - cur][:, nlo:ext], in0=P[cur][:, nlo:ext], in1=P[cur][:, nlo - lev:ext - lev],
                                op=ALU.add), reads=[Preg[cur]], writes=[Preg[1 - cur]])
                            cur, lev, lo = 1 - cur, lev * 2, nlo
                        S.add(en, lambda e, dv=dv, cur=cur, lev=lev, n=n, P=P: e.tensor_tensor(
                            out=dv, in0=P[cur][:, 16:16 + n], in1=P[cur][:, 16 - lev:16 + n - lev], op=ALU.add),
                            reads=[Preg[cur]], writes=dreg[dc])
                    if g == 0:
                        S.add(en, lambda e, dc=dc, gi=gi: e.tensor_tensor(
                            out=dT[dc][:, 0:16], in0=dT[dc][:, 0:16], in1=rvec.ap()[:, gi, :], op=ALU.mult),
                            reads=dreg[dc] + [("rvec", 0, 1)], writes=dreg[dc])
                else:
                    H = hT.ap()[:, dc, SOFF:SOFF + 368].rearrange("p (b j) -> p b j", j=23)
                    hreg = [(hk, SOFF, SOFF + 368)]
                    dv = dT[dc][:, 0:128].rearrange("p (b t) -> p b t", t=8)
                    if gi == 0:
                        S.add("dve", lambda e, dv=dv, H=H: e.tensor_tensor(out=dv, in0=H[:, :, 14:22], in1=H[:, :, 15:23],
                                                                           op=ALU.subtract), reads=hreg, writes=dreg[dc])
                    else:
                        Pv = [P[i][:, 0:368].rearrange("p (b j) -> p b j", j=23) for i in range(2)]
                        S.add(en, lambda e, Pv=Pv, H=H: e.tensor_tensor(out=Pv[0][:, :, 1:23], in0=H[:, :, 1:23],
                                                                           in1=H[:, :, 0:22], op=ALU.add),
                              reads=hreg, writes=[Preg[0]])
                        cur, lev, lo = 0, 2, 1
                        while lev < w:
                            nlo = lo + lev
                            S.add(en, lambda e, Pv=Pv, cur=cur, lev=lev, nlo=nlo: e.tensor_tensor(
                                out=Pv[1 - cur][:, :, nlo:23], in0=Pv[cur][:, :, nlo:23], in1=Pv[cur][:, :, nlo - lev:23 - lev],
                                op=ALU.add), reads=[Preg[cur]], writes=[Preg[1 - cur]])
                            cur, lev, lo = 1 - cur, lev * 2, nlo
                        S.add("dve", lambda e, dv=dv, H=H, Pv=Pv, cur=cur, w=w: e.scalar_tensor_tensor(
                            out=dv, in0=H[:, :, 15:23], scalar=-float(w), in1=Pv[cur][:, :, 15:23], op0=ALU.mult, op1=ALU.add),
                            reads=hreg + [Preg[cur]], writes=dreg[dc])
            if mid_group is not None:
                mid_group(g)
            for dcout in range(NDC):
                gi, oc = dcout // 2, dcout % 2
                bi = nxt("B", 3)
                pso = psB[bi].ap()[:, 0:n]
                for cc in range(2):
                    k_ = gi * 2 + cc
                    S.add("pe", lambda e, pso=pso, k_=k_, oc=oc, cc=cc, n=n, g=g: e.matmul(
                        pso, wv[:, k_, oc * 128:(oc + 1) * 128], dT[k_][:, 0:n],
                        start=(cc == 0), stop=(cc == 1 and g == 4)),
                        reads=[("aux4k", 0, 1)] + dreg[k_], writes=[(("psB", bi), 0, 512)])
                if g < 4:
                    for cc in range(2):
                        k_ = gi * 2 + cc
                        S.add("pe", lambda e, pso=pso, k_=k_, oc=oc, cc=cc, n=n, t0=t0: e.matmul(
                            pso, Wn[:, k_, oc * 128:(oc + 1) * 128], hT.ap()[:, k_, 16 + t0:16 + t0 + n],
                            start=False, stop=(cc == 1)),
                            reads=[SCR(8320 + k_ * 512, 8320 + (k_ + 1) * 512), (("hT", k_), 16 + t0, 16 + t0 + n)],
                            writes=[(("psB", bi), 0, 512)])
                add_residual(0, 2, (("psB", bi), 0, 512), pso, dcout, g)
            if after_group is not None:
                after_group(g)

    def pool_outputs():
        l, k = 0, 0
        pass

    def sgu_setup():
        w32 = scr.ap()[:, 0:1024].rearrange("p (g k t) -> p g k t", g=4, k=2)
        m32 = scr.ap()[:, 1024:1280].rearrange("p (k t) -> p k t", k=2)
        S.add("sp", lambda e: e.dma_start(out=w32, in_=d_wsp.ap()), writes=[SCR(0, 4096)], stream="s_wsp")
        S.add("sp", lambda e: e.dma_start(out=m32, in_=d_msk.ap()), writes=[SCR(4096, 5120)], stream="s_msk")
        for gi in range(4):
            S.add("dve", lambda e, gi=gi: e.tensor_tensor(out=wsp.ap()[:, gi, :, :], in0=w32[:, gi, :, :], in1=m32, op=ALU.mult),
                  reads=[SCR(0, 5120)], writes=[("wsp", gi, gi + 1)])
        S.add("pool", lambda e: e.dma_start(out=binv.ap(), in_=d_binv.ap()), writes=[("binv", 0, 1)], stream="s_binv")

    def sgu(after_sub=None, lazy=()):
        l = 1
        lazy = list(lazy)
        tail = []
        deferred = []
        banks4 = [(psA[0], ("psA", 0)), (psA[1], ("psA", 1)), (psA[2], ("psA", 2)), (psM, "psM")]
        rot["A4"] = 0
        vtm = scr_bf[:, 0:4096].rearrange("p (t c) -> p t c", t=4)
        vst = scr.ap()[:, 2048:3072]
        gbc = scr.ap()[:, 3072:4096]
        bbc = aux4k.ap().bitcast(F32)
        VST = SCR(8192, 12288)
        S.add("sp", lambda e: e.dma_start(out=gbc, in_=bass.AP(d_lng, 0, [[0, 128], [1, D]])), writes=[SCR(12288, 16384)], stream="s_lng")
        S.add("sp", lambda e: e.dma_start(out=bbc, in_=bass.AP(d_lnb, 0, [[0, 128], [1, D]])), writes=[("aux4k", 0, 1)], stream="s_lnb")
        pu = [pidx[("inu", 1, h)] for h in range(2)]
        pv_ = [pidx[("inv", 1, h)] for h in range(2)]
        po = [pidx[("out", 1, h)] for h in range(2)]
        L = lnst.ap()
        S.add("sp", lambda e: e.dma_start(out=btile.ap(), in_=bass.AP(d_bsp, 0, [[0, 128], [256, 4], [1, 128]])),
              writes=[("btile", 0, 1)], stream="s_bt")
        for g, (t0, n) in enumerate(GROUPS):
            c0, _ = hcols(g)
            sample = (g == 4)
            last = (g == len(GROUPS) - 1)
            ntile = n // 128
            if sample:
                uT = [scr_bf[:, 1024 + fc * 128:1024 + (fc + 1) * 128] for fc in range(NDC)]
                ureg = [[SCR(2048 + fc * 256, 2048 + (fc + 1) * 256)] for fc in range(NDC)]
            else:
                uT = [aT[fc // 4].ap()[:, fc % 4, 0:n] for fc in range(NDC)]
                ureg = [[(("aT", fc // 4), (fc % 4) * 512, (fc % 4 + 1) * 512)] for fc in range(NDC)]
            def emit_u(fcs, c0=c0, n=n, uT=uT, ureg=ureg):
                for fc in fcs:
                    wv = pview(pu[fc // 4])
                    bank, bkey = banks4[nxt("A4", 4)]
                    pso = bank.ap()[:, 0:n]
                    for dc in range(NDC):
                        S.add("pe", lambda e, pso=pso, wv=wv, dc=dc, fc=fc: e.matmul(
                            pso, wv[:, dc, (fc % 4) * 128:(fc % 4 + 1) * 128], hT.ap()[:, dc, c0:c0 + n],
                            start=(dc == 0), stop=(dc == NDC - 1)),
                            reads=[preg(pu[fc // 4]), (("hT", dc), c0, c0 + n)], writes=[(bkey, 0, 512)])
                    S.add("act", lambda e, pso=pso, fc=fc: e.activation(out=uT[fc], in_=pso, func=AF.Gelu,
                                                                        bias=vcol(40 + fc), scale=1.0),
                          reads=[(bkey, 0, 512), ("vecs", 0, 152)], writes=ureg[fc])
            for tt in range(ntile):
                out_tile = (t0 + tt * 128 == TP - 128) or sample
                for hf in range(2):
                    wv = pview(pv_[hf])
                    bank, bkey = banks4[nxt("A4", 4)]
                    pso = bank.ap()
                    for dc in range(NDC):
                        S.add("pe", lambda e, pso=pso, wv=wv, dc=dc, c0=c0, tt=tt: e.matmul(
                            pso, hT.ap()[:, dc, c0 + tt * 128:c0 + (tt + 1) * 128], wv[:, dc, :], start=(dc == 0), stop=False),
                            reads=[preg(pv_[hf]), (("hT", dc), c0, c0 + n)], writes=[(bkey, 0, 512)])
                    S.add("pe", lambda e, pso=pso, hf=hf: e.matmul(pso, ones.ap()[0:1, :], binv.ap()[0:1, hf * 512:(hf + 1) * 512],
                                                                   start=False, stop=True),
                          reads=[("ones", 0, 1), ("binv", 0, 1)], writes=[(bkey, 0, 512)])
                    S.add("act", lambda e, pso=pso, hf=hf: e.activation(out=vst[:, hf * 512:(hf + 1) * 512], in_=pso, func=AF.Gelu,
                                                                        accum_out=lnst.ap()[:, hf:hf + 1]),
                          reads=[(bkey, 0, 512)], writes=[SCR(8192 + hf * 2048, 8192 + (hf + 1) * 2048), ("lnst", hf, hf + 1)])
                    si2 = nxt("sq", 7)
                    S.add("act", lambda e, hf=hf, si2=si2: e.activation(out=sq[si2].ap(), in_=vst[:, hf * 512:(hf + 1) * 512], func=AF.Square,
                                                                       accum_out=lnst.ap()[:, 2 + hf:3 + hf]),
                          reads=[SCR(8192 + hf * 2048, 8192 + (hf + 1) * 2048)], writes=[(("sq", si2), 0, 512), ("lnst", 2 + hf, 3 + hf)])
                if last and tt == ntile - 1:
                    done_piece(pv_[0]); done_piece(pv_[1])
                S.add("dve", lambda e: e.tensor_tensor(out=L[:, 4:5], in0=L[:, 0:1], in1=L[:, 1:2], op=ALU.add),
                      reads=[("lnst", 0, 2)], writes=[("lnst", 4, 5)])
                S.add("dve", lambda e: e.tensor_tensor(out=L[:, 5:6], in0=L[:, 2:3], in1=L[:, 3:4], op=ALU.add),
                      reads=[("lnst", 2, 4)], writes=[("lnst", 5, 6)])
                S.add("dve", lambda e: e.tensor_scalar(out=L[:, 4:6], in0=L[:, 4:6], scalar1=1.0 / D, scalar2=None, op0=ALU.mult),
                      reads=[("lnst", 4, 6)], writes=[("lnst", 4, 6)])
                S.add("dve", lambda e: e.tensor_tensor(out=L[:, 6:7], in0=L[:, 4:5], in1=L[:, 4:5], op=ALU.mult),
                      reads=[("lnst", 4, 5)], writes=[("lnst", 6, 7)])
                S.add("dve", lambda e: e.scalar_tensor_tensor(out=L[:, 7:8], in0=L[:, 5:6], scalar=float(EPS), in1=L[:, 6:7],
                                                              op0=ALU.add, op1=ALU.subtract),
                      reads=[("lnst", 5, 7)], writes=[("lnst", 7, 8)])
                S.add("pool", lambda e: e.tensor_tensor(out=L[:, 8:9], in0=L[:, 7:8], in1=mhalf.ap(), op=ALU.pow),
                      reads=[("lnst", 7, 8), ("mhalf", 0, 1)], writes=[("lnst", 8, 9)])
                vreg = SCR(tt * 2048, (tt + 1) * 2048)
                if not out_tile:
                    S.add("dve", lambda e, tt=tt: e.tensor_scalar(out=vtm[:, tt, :], in0=vst, scalar1=L[:, 4:5], scalar2=L[:, 8:9],
                                                               op0=ALU.subtract, op1=ALU.mult),
                          reads=[VST, ("lnst", 4, 9)], writes=[vreg])
                    S.add("dve", lambda e, tt=tt: e.tensor_tensor(out=vtm[:, tt, :], in0=vtm[:, tt, :], in1=gbc, op=ALU.mult),
                          reads=[vreg, SCR(12288, 16384)], writes=[vreg])
                    S.add("dve", lambda e, tt=tt: e.tensor_tensor(out=vtm[:, tt, :], in0=vtm[:, tt, :], in1=bbc, op=ALU.add),
                          reads=[vreg, ("aux4k", 0, 1)], writes=[vreg])
                else:
                    S.add("dve", lambda e: e.tensor_scalar(out=vst, in0=vst, scalar1=L[:, 4:5], scalar2=L[:, 8:9],
                                                           op0=ALU.subtract, op1=ALU.mult),
                          reads=[VST, ("lnst", 4, 9)], writes=[VST])
                    S.add("dve", lambda e: e.tensor_tensor(out=vst, in0=vst, in1=gbc, op=ALU.mult),
                          reads=[VST, SCR(12288, 16384)], writes=[VST])
                if out_tile:
                    S.add("dve", lambda e: e.tensor_tensor(out=vst, in0=vst, in1=bbc, op=ALU.add),
                          reads=[VST, ("aux4k", 0, 1)], writes=[VST])
                    S.add("dve", lambda e, tt=tt: e.tensor_copy(out=vtm[:, tt, :], in_=vst), reads=[VST], writes=[vreg])
                    od = o_vS if sample else o_vP
                    S.add("sp", lambda e, od=od: e.dma_start(out=od.ap(), in_=vst), reads=[VST], stream="o_v%d" % int(sample))
                emit_u(range(8) if ntile == 1 else [[0], [1], [2, 3], [4, 5, 6, 7]][tt])
            if last:
                done_piece(pu[0]); done_piece(pu[1])
            def part_b(g=g, t0=t0, n=n, c0=c0, sample=sample, last=last, ntile=ntile, uT=uT, ureg=ureg):
                kk = 1 if sample else 0
                if sample:
                    S.add("sp", lambda e: e.dma_start(out=btile.ap(), in_=bass.AP(d_bsp, 128, [[0, 128], [256, 4], [1, 128]])),
                          writes=[("btile", 0, 1)], stream="s_bt")
                for cc in range(NDC):
                    gi = cc // 2
                    bank, bkey = banks4[nxt("A4", 4)]
                    for tt in range(ntile):
                        pso = bank.ap()[:, tt * 128:(tt + 1) * 128]
                        S.add("pe", lambda e, pso=pso, tt=tt, cc=cc, gi=gi, kk=kk: e.matmul(
                            pso, vtm[:, tt, cc * 128:(cc + 1) * 128], wsp.ap()[:, gi, kk, :], start=True, stop=True),
                            reads=[SCR(tt * 2048, (tt + 1) * 2048), ("wsp", gi, gi + 1)], writes=[(bkey, 0, 512)])
                    bt = btile.ap()[:, gi, :]
                    btb = bass.AP(bt.tensor, bt.offset, [list(bt.ap[0]), [0, ntile], [1, 128]])
                    p3 = bank.ap()[:, 0:n].rearrange("p (a b) -> p a b", b=128)
                    S.add("dve", lambda e, p3=p3, btb=btb: e.tensor_tensor(out=p3, in0=p3, in1=btb, op=ALU.add),
                          reads=[(bkey, 0, 512), ("btile", 0, 1)], writes=[(bkey, 0, 512)])
                    S.add("dve", lambda e, bank=bank, cc=cc, n=n, uT=uT: e.tensor_tensor(
                        out=uT[cc], in0=bank.ap()[:, 0:n], in1=uT[cc], op=ALU.mult),
                        reads=[(bkey, 0, 512)] + ureg[cc], writes=ureg[cc])
                for dco in range(NDC):
                    wv = pview(po[dco // 4])
                    bi = nxt("B", 3)
                    pso = psB[bi].ap()[:, 0:n]
                    for cc in range(NDC):
                        S.add("pe", lambda e, pso=pso, wv=wv, dco=dco, cc=cc, uT=uT: e.matmul(
                            pso, wv[:, cc, (dco % 4) * 128:(dco % 4 + 1) * 128], uT[cc], start=(cc == 0), stop=(cc == NDC - 1)),
                            reads=[preg(po[dco // 4])] + ureg[cc], writes=[(("psB", bi), 0, 512)])
                    xs = xT.ap()[:, dco, t0:t0 + n]
                    xr = [(("xT", dco), t0, t0 + n)]
                    bk = (("psB", bi), 0, 512)
                    if not sample:
                        S.add("dve", lambda e, xs=xs, pso=pso, dco=dco: e.scalar_tensor_tensor(
                            out=xs, in0=pso, scalar=modT.ap()[:, 1, DER[2], dco, 0:1], in1=xs, op0=ALU.mult, op1=ALU.add),
                            reads=xr + [bk, (("modT", 1, DER[2]), dco, dco + 1)], writes=xr)
                    else:
                        tmp = scr.ap()[:, 1024:1152].rearrange("p (b t) -> p b t", t=8)
                        S.add("dve", lambda e, pso=pso, dco=dco, tmp=tmp: e.tensor_tensor(
                            out=tmp, in0=pso.rearrange("p (b t) -> p b t", t=8), in1=bc_seq(modT.ap(), 1, DER[2], dco), op=ALU.mult),
                            reads=[bk, (("modT", 1, DER[2]), dco, dco + 1)], writes=[SCR(4096, 4608)])
                        S.add("dve", lambda e, xs=xs, tmp=tmp: e.tensor_tensor(
                            out=xs.rearrange("p (b t) -> p b t", t=8), in0=xs.rearrange("p (b t) -> p b t", t=8), in1=tmp, op=ALU.add),
                            reads=xr + [SCR(4096, 4608)], writes=xr)
                if last:
                    done_piece(po[0]); done_piece(po[1])
            if sample:
                deferred.append(part_b)
            else:
                part_b()
            if lazy:
                lazy.pop(0)()
            if after_sub is not None:
                tail.append((g + 1, lambda g=g: after_sub(g, "stats_a")))
                tail.append((g + 2, lambda g=g: (after_sub(g, "stats_b"), after_sub(g, "mod"))))
                tail.sort(key=lambda t: t[0])
                while tail and tail[0][0] <= g:
                    tail.pop(0)[1]()
        rest_ = [f_ for (_, f_) in tail]
        return rest_[:1] + deferred + rest_[1:]


    for a_ in range(4):
        ada_piece(0, a_)
    ada_derive_A(0, 0)
    rest = list(range(4, 12))
    norm_group(0, 0, 0, pool_layout=True, part="stats")
    for g in range(5):
        if g + 1 < 5:
            norm_group(0, 0, g + 1, pool_layout=True, part="stats")
        norm_group(0, 0, g, pool_layout=True, part="mod", shift_eng=("act" if g % 2 == 0 else "dve"))
        for _ in range(2):
            if rest:
                ada_piece(0, rest.pop(0))
    while rest:
        ada_piece(0, rest.pop(0))
    ada_derive_G1(0)
    ada_derive_A(0, 1)
    ada_derive_G2(0)
    hp = scr.ap()[:, 3584:3712].rearrange("p (dc t) -> p dc t", t=16)
    for dc in range(NDC):
        xs = xT.ap()[:, dc, TP - 16:TP]
        S.add("dve", lambda e, dc=dc, xs=xs: e.scalar_tensor_tensor(
            out=hp[:, dc, :], in0=xs, scalar=modT.ap()[:, 0, DER[0], dc, 0:1], in1=rstd[1].ap()[:, 496:512], op0=ALU.mult, op1=ALU.mult),
            reads=[(("xT", dc), TP - 16, TP), (("rstd", 1), 0, 512), (("modT", 0, DER[0]), dc, dc + 1)], writes=[SCR(14336 + dc * 64, 14400 + dc * 64)])
        S.add("dve", lambda e, dc=dc: e.tensor_scalar(out=hp[:, dc, :], in0=hp[:, dc, :], scalar1=modT.ap()[:, 0, 0, dc, 0:1],
                                                      scalar2=None, op0=ALU.add),
              reads=[SCR(14336 + dc * 64, 14400 + dc * 64), (("modT", 0, 0), dc, dc + 1)], writes=[SCR(14336 + dc * 64, 14400 + dc * 64)])
        S.add("sp", lambda e, dc=dc: e.dma_start(out=o_npp.ap()[dc * 128:(dc + 1) * 128, :], in_=hp[:, dc, 1:16]),
              reads=[SCR(14336 + dc * 64, 14400 + dc * 64)], stream="o_npp")
        h32 = scr.ap()[:, dc * 128:(dc + 1) * 128].rearrange("p (b t) -> p b t", t=8)
        S.add("sp", lambda e, dc=dc, h32=h32: e.dma_start(out=o_npsb.ap()[dc * 128:(dc + 1) * 128, :].rearrange("p (b t) -> p b t", t=8), in_=h32),
              reads=[SCR(dc * 512, (dc + 1) * 512)], stream="o_nps1")

    ada1 = list(range(12))

    def pm_mid(g):
        if 1 <= g <= 3:
            norm_group(0, 1, g - 1, shift_eng="act")

    def pm_after(g):
        for _ in range(3):
            if ada1:
                ada_piece(1, ada1.pop(0))
    pool_mixer(after_group=pm_after, mid_group=pm_mid)
    while ada1:
        ada_piece(1, ada1.pop(0))
    ada_derive_A(1, 0)
    ada_derive_G1(1)
    ada_derive_A(1, 1)
    ada_derive_G2(1)

    def mlp0_after_blk(blk):
        if blk == 6:
            sgu_setup()
    tail0 = mlp(0, after_blk=mlp0_after_blk, after_last=lambda g, part: norm_group(1, 0, g, part=part),
                lazy=[lambda: norm_group(0, 1, 3, part="stats"),
                      lambda: (norm_group(0, 1, 3, part="mod"), norm_group(0, 1, 4, part="stats")),
                      lambda: norm_group(0, 1, 4, part="mod")], flush=False)

    tail1 = sgu(after_sub=lambda g, part: norm_group(1, 1, g, part=part), lazy=tail0)

    def final_group(g, part):
        norm_group(0, 0, g, final=True, part=part)
        if part == "stats":
            return
        t0, n = GROUPS[g]
        if g >= 0:
            S.add("sp", lambda e: e.dma_start(out=o_yT.ap()[:, t0:t0 + n].rearrange("(dc p) t -> p dc t", p=128),
                                              in_=xT.ap()[:, :, t0:t0 + n]),
                  reads=[(("xT", dc), t0, t0 + n) for dc in range(NDC)], stream="o_yg%d" % g)
            return
        for dc in range(NDC):
            S.add("sp", lambda e, dc=dc: e.dma_start(out=o_yT.ap()[dc * 128:(dc + 1) * 128, t0:t0 + n], in_=xT.ap()[:, dc, t0:t0 + n]),
                  reads=[(("xT", dc), t0, t0 + n)], stream="o_y%d" % dc)
    mlp(1, after_last=final_group, lazy=tail1)
    S.emit_all()
    print("sched stats:", S.stats)
    return nc


_CACHE = {}


def kernel(x_prompt, x_sample, c_prompt, c_sample, state_pool, norm_g, w_ada, b_ada, w_pool, pool_scale,
           sgu_w_in, sgu_b_in, sgu_ln_g, sgu_ln_b, sgu_w_sp, sgu_b_sp, sgu_w_out, mlp_w1, mlp_w2, final_g):
    f = lambda a: np.ascontiguousarray(np.asarray(a), dtype=np.float32)
    x_prompt, x_sample, c_prompt, c_sample, state_pool = map(f, (x_prompt, x_sample, c_prompt, c_sample, state_pool))
    norm_g, w_ada, b_ada, w_pool, pool_scale = map(f, (norm_g, w_ada, b_ada, w_pool, pool_scale))
    sgu_w_in, sgu_b_in, sgu_ln_g, sgu_ln_b, sgu_w_sp, sgu_b_sp, sgu_w_out = map(
        f, (sgu_w_in, sgu_b_in, sgu_ln_g, sgu_ln_b, sgu_w_sp, sgu_b_sp, sgu_w_out))
    mlp_w1, mlp_w2, final_g = map(f, (mlp_w1, mlp_w2, final_g))

    fm = lambda v: v.reshape(-1, 128).T
    vecs = np.concatenate([fm(norm_g.reshape(-1)), fm(pool_scale.reshape(-1)), fm(sgu_b_in[0, :D]), fm(final_g),
                           fm(b_ada.reshape(-1))], axis=1)
    assert vecs.shape == (128, 152)
    wsp_pack = np.zeros((128, 4, 2, 128), np.float32)
    bsp_pack = np.zeros((1, 4, 2, 128), np.float32)
    for gi in range(4):
        wsp_pack[:, gi, 0, :] = sgu_w_sp[0, gi].T
        wsp_pack[:, gi, 1, :] = np.tile(sgu_w_sp[0, gi][:8, :8].T, (16, 16))
        bsp_pack[0, gi, 0, :] = sgu_b_sp[0, gi]
        bsp_pack[0, gi, 1, :] = np.tile(sgu_b_sp[0, gi][:8], 16)
    s_idx = np.arange(128)[:, None]
    t_idx = np.arange(128)[None, :]
    msk = np.zeros((128, 2, 128), np.float32)
    msk[:, 0, :] = (s_idx <= t_idx)
    msk[:, 1, :] = (s_idx <= t_idx) & ((s_idx // 8) == (t_idx // 8))
    rv = np.ones((128, 4, 16), np.float32)
    for gi in range(4):
        w = 2 ** (gi + 1)
        rv[:, gi, :] = (w / np.minimum(w, np.arange(16) + 1.0))[None, :]

    shared = {
        "vecs": np.ascontiguousarray(vecs), "w_ada": w_ada, "w_pool": w_pool[0], "sgu_w_in": sgu_w_in[0],
        "sgu_w_out": sgu_w_out[0], "mlp_w1": mlp_w1, "mlp_w2": mlp_w2, "wsp_pack": wsp_pack, "msk_pack": msk,
        "bsp_pack": bsp_pack, "binv": np.ascontiguousarray(sgu_b_in[:, D:]), "ln_g": sgu_ln_g, "ln_b": sgu_ln_b, "rvec": rv,
    }
    in_maps = []
    for i in range(8):
        xs = x_sample[16 * i:16 * i + 16].reshape(128, D)
        m = dict(shared)
        m["xT"] = np.ascontiguousarray(np.concatenate([x_prompt[i].T, xs.T], axis=1))
        m["cT"] = np.ascontiguousarray(np.concatenate([c_prompt[i][:, None], c_sample[16 * i:16 * i + 16].T], axis=1))
        spt = state_pool[0, 16 * i:16 * i + 16].transpose(2, 0, 1)
        m["spT"] = np.ascontiguousarray(np.pad(spt, ((0, 0), (0, 0), (0, 8))).reshape(D, 368))
        m["sp_tail"] = np.ascontiguousarray(spt[:, :, 8:15].reshape(D, 112))
        in_maps.append(m)

    if "nc" not in _CACHE:
        _CACHE["nc"] = build_program()
    res = run_bass_kernel_spmd(_CACHE["nc"], in_maps, core_ids=list(range(8)))
    R = res.results
    y_prompt = np.stack([R[i]["yT"][:, :TP].T for i in range(8)]).astype(np.float32)
    y_sample = np.concatenate([R[i]["yT"][:, TP:].T.reshape(16, 8, D) for i in range(8)]).astype(np.float32)
    npp = np.stack([R[i]["nppT"].T for i in range(8)])[None].astype(np.float32)
    nps = np.concatenate([np.concatenate([R[i]["npsTa"].reshape(D, 16, 7), R[i]["npsTb"].reshape(D, 16, 8)], axis=2).transpose(1, 2, 0)
                          for i in range(8)])[None].astype(np.float32)
    vP = np.stack([R[i]["vP"] for i in range(8)])[None].astype(np.float32)
    vS = np.concatenate([R[i]["vS"].reshape(16, 8, D) for i in range(8)])[None].astype(np.float32)
    return (np.ascontiguousarray(y_prompt), np.ascontiguousarray(y_sample), np.ascontiguousarray(npp),
            np.ascontiguousarray(nps), np.ascontiguousarray(vP), np.ascontiguousarray(vS))
```

```python
import numpy as np
import concourse.bass as bass
import concourse.mybir as mybir
from concourse.bass_utils import run_bass_kernel_spmd

F32 = mybir.dt.float32
BF16 = mybir.dt.bfloat16
ALU = mybir.AluOpType
AF = mybir.ActivationFunctionType

D = 1024
NDC = 8
TP = 2048
TS = 128
T = TP + TS
NSEQ = 17
DFF = 4096
EPS = 1e-6
HW = 2432
SOFF = 2064
NS = 6
GROUPS = [(0, 512), (512, 512), (1024, 512), (1536, 512), (2048, 128)]

ENGS = ("pe", "act", "dve", "pool", "sp")


class Op:
    __slots__ = ("eng", "emit", "deps", "signal", "sigval", "stream", "idx")

    def __init__(self, eng, emit, stream):
        self.eng = eng
        self.emit = emit
        self.deps = []
        self.signal = False
        self.sigval = None
        self.stream = stream


class Sched:
    def __init__(self, nc):
        self.nc = nc
        self.ops = {e: [] for e in ENGS}
        self.rec = {}
        self.nops = 0

    def add(self, eng, emit, reads=(), writes=(), stream=None):
        op = Op(eng, emit, stream)
        op.idx = self.nops
        self.nops += 1
        deps = {}

        def add_dep(d):
            if d.stream is None and d.eng == "pe" and eng == "pe":
                return
            k = ("dma", id(d)) if d.stream is not None else ("eng", d.eng)
            cur = deps.get(k)
            if cur is None or d.idx > cur.idx:
                deps[k] = d

        for (key, lo, hi) in reads:
            for r in self.rec.get(key, ()):
                if r[2] == "w" and r[0] < hi and lo < r[1]:
                    add_dep(r[3])
        for (key, lo, hi) in writes:
            for r in self.rec.get(key, ()):
                if r[0] < hi and lo < r[1]:
                    add_dep(r[3])
        for (key, lo, hi) in writes:
            lst = self.rec.setdefault(key, [])
            lst[:] = [r for r in lst if not (lo <= r[0] and r[1] <= hi)]
            lst.append([lo, hi, "w", op])
        for (key, lo, hi) in reads:
            lst = self.rec.setdefault(key, [])
            if stream is None:
                lst[:] = [r for r in lst if not (r[2] == "r" and r[3].eng == eng and r[3].stream is None
                                                 and lo <= r[0] and r[1] <= hi)]
            lst.append([lo, hi, "r", op])
        op.deps = list(deps.values())
        for d in op.deps:
            d.signal = True
        self.ops[eng].append(op)
        return op

    def emit_all(self):
        nc = self.nc
        esem = {e: nc.alloc_semaphore("sem_" + e) for e in ENGS}
        ssem, scount = {}, {}
        ecount = {e: 0 for e in ENGS}
        for e in ENGS:
            for op in self.ops[e]:
                if op.stream is not None:
                    if op.stream not in ssem:
                        ssem[op.stream] = nc.alloc_semaphore("dma_" + op.stream)
                        scount[op.stream] = 0
                    scount[op.stream] += 16
                    op.sigval = (ssem[op.stream], scount[op.stream])
                elif op.signal:
                    ecount[e] += 1
                    op.sigval = (esem[e], ecount[e])
        self.stats = {e: (len(self.ops[e]), ecount[e]) for e in ENGS}
        handles = {"pe": "tensor", "act": "scalar", "dve": "vector", "pool": "gpsimd", "sp": "sync"}
        final = dict((s, (ssem[s], scount[s])) for s in ssem)
        with nc.Block() as block:
            for e in ENGS:
                def body(eng, ops=self.ops[e], e=e):
                    waited = {}
                    for op in ops:
                        need = {}
                        for d in op.deps:
                            sem, val = d.sigval
                            k = id(sem)
                            if k not in need or need[k][1] < val:
                                need[k] = (sem, val)
                        for k, (sem, val) in need.items():
                            if waited.get(k, 0) >= val:
                                continue
                            eng.wait_ge(sem, val)
                            waited[k] = val
                        ins = op.emit(eng)
                        if op.stream is not None:
                            ins.then_inc(op.sigval[0], 16)
                        elif op.signal:
                            ins.then_inc(op.sigval[0], 1)
                    if e == "sp":
                        for s, (sem, val) in final.items():
                            if waited.get(id(sem), 0) < val:
                                eng.wait_ge(sem, val)
                        for e2 in ENGS:
                            if e2 != e and ecount[e2] > 0:
                                eng.wait_ge(esem[e2], ecount[e2])

                getattr(block, handles[e])(body)


def build_program():
    nc = bass.Bass("TRN2", target_bir_lowering=False)
    S = Sched(nc)

    def din(name, shape):
        return nc.dram_tensor(name, list(shape), F32, kind="ExternalInput")

    def dout(name, shape):
        return nc.dram_tensor(name, list(shape), F32, kind="ExternalOutput")

    d_xT = din("xT", [D, T])
    d_cT = din("cT", [D, NSEQ])
    d_spT = din("spT", [D, 368])
    d_spt = din("sp_tail", [D, 112])
    d_vecs = din("vecs", [128, 152])
    d_wada = din("w_ada", [2, D, 6 * D])
    d_wpool = din("w_pool", [4, 256, 256])
    d_win = din("sgu_w_in", [D, 2 * D])
    d_wout = din("sgu_w_out", [D, D])
    d_w1 = din("mlp_w1", [2, D, DFF])
    d_w2 = din("mlp_w2", [2, DFF, D])
    d_wsp = din("wsp_pack", [128, 4, 2, 128])
    d_msk = din("msk_pack", [128, 2, 128])
    d_bsp = din("bsp_pack", [1, 4, 2, 128])
    d_binv = din("binv", [1, D])
    d_lng = din("ln_g", [1, D])
    d_lnb = din("ln_b", [1, D])
    d_rvec = din("rvec", [128, 4, 16])
    o_yT = dout("yT", [D, T])
    o_npp = dout("nppT", [D, 15])
    o_npsa = dout("npsTa", [D, 112])
    o_npsb = dout("npsTb", [D, 128])
    o_vP = dout("vP", [128, D])
    o_vS = dout("vS", [128, D])

    def sb(name, shape, dt):
        return nc.alloc_sbuf_tensor(name, list(shape), dt)

    xT = sb("xT_sb", [128, NDC, T], F32)
    hT = sb("hT_sb", [128, NDC, HW], BF16)
    ring = [sb("ring%d" % i, [128, 4096], BF16) for i in range(NS)]
    aT = [sb("aT%d" % i, [128, 4, 512], BF16) for i in range(2)]
    scr = sb("scr", [128, 4096], F32)
    rstd = [sb("rstd%d" % i, [128, 512], F32) for i in range(2)]
    sq = [sb("sq%d" % i, [128, 512], BF16) for i in range(7)]
    modT = sb("modT", [128, 2, 6, NDC, NSEQ], F32)
    DER = {0: 1, 1: 4, 2: 2, 3: 5}
    vecs = sb("vecs_sb", [128, 152], F32)
    ones = sb("ones", [128, 128], BF16)
    cT = sb("cT_sb", [128, NDC, NSEQ], F32)
    siluT = sb("siluT", [128, NDC, NSEQ], BF16)
    wsp = sb("wsp", [128, 4, 2, 128], BF16)
    btile = sb("btile", [128, 4, 128], F32)
    binv = sb("binv_sb", [1, D], BF16)
    lnst = sb("lnst", [128, 16], F32)
    mhalf = sb("mhalf", [128, 1], F32)
    epsd = sb("epsd", [128, 1], F32)
    aux4k = sb("aux4k", [128, 2048], BF16)
    rvec = sb("rvec_sb", [128, 4, 16], F32)
    print("sbuf bytes remaining:", nc.sbuf_bytes_remaining)

    psA = [nc.alloc_psum_tensor("psA%d" % i, [128, 512], F32) for i in range(3)]
    psB = [nc.alloc_psum_tensor("psB%d" % i, [128, 512], F32) for i in range(3)]
    psS = nc.alloc_psum_tensor("psS", [128, 512], F32)
    psM = nc.alloc_psum_tensor("psM", [128, 512], F32)
    rot = {"A": 0, "A4m": 0, "B": 0, "B4": 0, "M": 0, "sq": 0, "rstd": 0}

    def nxt(kind, n):
        v = rot[kind]
        rot[kind] = (v + 1) % n
        return v

    scr_bf = scr.ap().bitcast(BF16)

    def SCR(lo_b, hi_b):
        return ("scr", lo_b, hi_b)

    S.add("sp", lambda e: e.dma_start(out=vecs.ap(), in_=d_vecs.ap()), writes=[("vecs", 0, 152)], stream="s_vecs")
    S.add("sp", lambda e: e.dma_start(out=cT.ap(), in_=d_cT.ap().rearrange("(dc p) b -> p dc b", p=128)),
          writes=[("cT", 0, 1)], stream="s_cT")
    S.add("sp", lambda e: e.dma_start(out=rvec.ap(), in_=d_rvec.ap()), writes=[("rvec", 0, 1)], stream="s_rvec")
    S.add("dve", lambda e: e.memset(ones.ap(), 1.0), writes=[("ones", 0, 1)])
    S.add("dve", lambda e: e.tensor_scalar(out=vecs.ap()[:, 48:56], in0=vecs.ap()[:, 48:56], scalar1=float(np.sqrt(D)), scalar2=None,
                                           op0=ALU.mult), reads=[("vecs", 0, 152)], writes=[("vecs", 0, 152)])
    S.add("dve", lambda e: e.tensor_scalar(out=vecs.ap()[:, 0:32], in0=vecs.ap()[:, 0:32], scalar1=float(np.sqrt(D)), scalar2=None,
                                           op0=ALU.mult), reads=[("vecs", 0, 152)], writes=[("vecs", 0, 152)])
    S.add("dve", lambda e: e.memset(mhalf.ap(), -0.5), writes=[("mhalf", 0, 1)])
    S.add("dve", lambda e: e.memset(epsd.ap(), float(D * EPS)), writes=[("epsd", 0, 1)])
    for dc in range(NDC):
        S.add("dve", lambda e, dc=dc: e.memset(hT.ap()[:, dc, 0:16], 0.0), writes=[(("hT", dc), 0, 16)])
    S.add("act", lambda e: e.activation(out=siluT.ap(), in_=cT.ap(), func=AF.Silu),
          reads=[("cT", 0, 1)], writes=[("siluT", 0, 1)])

    pieces = []

    def colpiece(src2d, c0, ncols=512):
        return src2d[:, c0:c0 + ncols].rearrange("(dc p) j -> p dc j", p=128), (NDC, ncols)

    def rowpiece(src2d, r0):
        return src2d[r0:r0 + 512, :].rearrange("(fc p) j -> p fc j", p=128), (4, 1024)

    for a in range(12):
        pieces.append(("ada", 0, a) + colpiece(d_wada.ap()[0], a * 512))
    for a in range(12):
        pieces.append(("ada", 1, a) + colpiece(d_wada.ap()[1], a * 512))
    for blk in range(8):
        pieces.append(("w1", 0, blk) + colpiece(d_w1.ap()[0], blk * 512))
        pieces.append(("w2", 0, blk) + rowpiece(d_w2.ap()[0], blk * 512))
    for h in range(2):
        pieces.append(("inu", 1, h) + colpiece(d_win.ap(), h * 512))
    for h in range(2):
        pieces.append(("inv", 1, h) + colpiece(d_win.ap(), D + h * 512))
    for h in range(2):
        pieces.append(("out", 1, h) + colpiece(d_wout.ap(), h * 512))
    for blk in range(8):
        pieces.append(("w1", 1, blk) + colpiece(d_w1.ap()[1], blk * 512))
        pieces.append(("w2", 1, blk) + rowpiece(d_w2.ap()[1], blk * 512))
    pidx = {(p[0], p[1], p[2]): i for i, p in enumerate(pieces)}
    state = {"issued": 0}

    def pview(i):
        assert i < state["issued"], "ring piece %d consumed before its DMA was issued" % i
        a, b = pieces[i][4]
        return ring[i % NS].ap()[:, 0:a * b].rearrange("p (a b) -> p a b", b=b)

    def preg(i):
        return ("ring", i % NS), 0, 4096

    consumed = set()

    def issue_ready(limit=None, extra_reads=()):
        while state["issued"] < (len(pieces) if limit is None else limit):
            i = state["issued"]
            if i >= NS and (i - NS) not in consumed:
                return
            state["issued"] += 1
            src = pieces[i][3]
            dst = pview(i)
            S.add("pool", lambda e, dst=dst, src=src: e.dma_start(out=dst, in_=src), writes=[preg(i)], reads=list(extra_reads),
                  stream="ring%d" % (i % NS))

    def done_piece(i):
        consumed.add(i)
        issue_ready()

    issue_ready(limit=4)
    for g, (t0, n) in enumerate(GROUPS):
        if g == 2:
            issue_ready(limit=NS, extra_reads=[(("xT", 0), 512, 1024)])
        S.add("sp", lambda e, t0=t0, n=n: e.dma_start(out=xT.ap()[:, :, t0:t0 + n],
                                                      in_=d_xT.ap()[:, t0:t0 + n].rearrange("(dc p) t -> p dc t", p=128)),
              writes=[(("xT", dc), t0, t0 + n) for dc in range(NDC)], stream="s_x%d" % g,
              reads=([(("ring", 3), 0, 4096)] if g >= 2 else []))
    issue_ready()
    S.add("pool", lambda e: e.dma_start(out=aux4k.ap().rearrange("p (a b) -> p a b", b=256),
                                        in_=d_wpool.ap().rearrange("g (cc p) j -> p (g cc) j", p=128)),
          writes=[("aux4k", 0, 1)], stream="s_wpool")
    S.add("pool", lambda e: e.dma_start(out=hT.ap()[:, :, SOFF:SOFF + 368], in_=d_spT.ap().rearrange("(dc p) j -> p dc j", p=128)),
          writes=[(("hT", dc), SOFF, SOFF + 368) for dc in range(NDC)], stream="s_sp")
    S.add("sp", lambda e: e.dma_start(out=o_npsa.ap(), in_=d_spt.ap()), stream="o_nps0")

    def vcol(c):
        return vecs.ap()[:, c:c + 1]

    def ada_piece(l, a):
        pi = pidx[("ada", l, a)]
        wv = pview(pi)
        bank, bkey = psM, "psM"
        for jj in range(4):
            pso = bank.ap()[:, jj * 32: jj * 32 + NSEQ]
            for dc in range(NDC):
                S.add("pe", lambda e, pso=pso, wv=wv, dc=dc, jj=jj: e.matmul(
                    pso, wv[:, dc, jj * 128:(jj + 1) * 128], siluT.ap()[:, dc, :], start=(dc == 0), stop=(dc == NDC - 1)),
                    reads=[preg(pi), ("siluT", 0, 1)], writes=[(bkey, 0, 512)])
        for jj in range(4):
            jc = a * 4 + jj
            pso = bank.ap()[:, jj * 32: jj * 32 + NSEQ]
            v, dcj = jc // 8, jc % 8
            S.add("act", lambda e, pso=pso, l=l, v=v, dcj=dcj, jc=jc: e.activation(
                out=modT.ap()[:, l, v, dcj, :], in_=pso, func=AF.Identity, bias=vcol(56 + l * 48 + jc), scale=1.0),
                reads=[(bkey, 0, 512), ("vecs", 0, 152)],
                writes=[(("modT", l, v), dcj, dcj + 1)])
        done_piece(pi)

    def ada_derive_A(l, k):
        vsc, nrm = [(1, 0), (4, 1)][k]
        for dc in range(NDC):
            S.add("dve", lambda e, dc=dc: e.tensor_scalar(
                out=modT.ap()[:, l, DER[k], dc, :], in0=modT.ap()[:, l, vsc, dc, :], scalar1=1.0,
                scalar2=vcol((l * 2 + nrm) * 8 + dc), op0=ALU.add, op1=ALU.mult),
                reads=[(("modT", l, vsc), dc, dc + 1), ("vecs", 0, 152)], writes=[(("modT", l, DER[k]), dc, dc + 1)])

    def ada_derive_G1(l):
        for dc in range(NDC):
            if l == 0:
                w = float(2 ** (dc // 2 + 1))
                S.add("dve", lambda e, dc=dc, w=w: e.tensor_scalar(
                    out=modT.ap()[:, 0, DER[2], dc, :], in0=modT.ap()[:, 0, 2, dc, :], scalar1=vcol(32 + dc), scalar2=1.0 / w,
                    op0=ALU.mult, op1=ALU.mult),
                    reads=[(("modT", 0, 2), dc, dc + 1), ("vecs", 0, 152)], writes=[(("modT", 0, DER[2]), dc, dc + 1)])

    def ada_derive_G2(l):
        pass

    def bc_seq(t_ap5, l, k, dc):
        base = t_ap5[:, l, k, dc, 1:2]
        return bass.AP(base.tensor, base.offset, [list(base.ap[0]), [1, 16], [0, 8]])

    def hcols(g):
        t0, n = GROUPS[g]
        if g < 4:
            return 16 + t0, n
        return SOFF, n

    norm_state = {}

    def norm_group(l, k, g, pool_layout=False, final=False, part="all", shift_eng="dve"):
        vA = k
        vsh = 0 if k == 0 else 3
        for (t0, n) in [GROUPS[g]]:
          if part in ("all", "stats", "stats_a"):
            for dc in range(NDC):
                si = nxt("sq", 7)
                S.add("act", lambda e, si=si, dc=dc, t0=t0, n=n: e.activation(
                    out=sq[si].ap()[:, 0:n], in_=xT.ap()[:, dc, t0:t0 + n], func=AF.Square),
                    reads=[(("xT", dc), t0, t0 + n)], writes=[(("sq", si), 0, 512)])
                S.add("pe", lambda e, si=si, dc=dc, n=n: e.matmul(
                    psS.ap()[:, 0:n], ones.ap(), sq[si].ap()[:, 0:n], start=(dc == 0), stop=(dc == NDC - 1)),
                    reads=[(("sq", si), 0, 512), ("ones", 0, 1)], writes=[("psS", 0, 512)])
          if part in ("all", "stats", "stats_b"):
            ri = nxt("rstd", 2)
            R = rstd[ri].ap()[:, 0:n]
            S.add("act", lambda e, R=R, n=n: e.activation(out=R, in_=psS.ap()[:, 0:n], func=AF.Ln, bias=epsd.ap()[:, 0:1], scale=1.0),
                  reads=[("psS", 0, 512), ("epsd", 0, 1)], writes=[(("rstd", ri), 0, 512)])
            S.add("act", lambda e, R=R: e.activation(out=R, in_=R, func=AF.Exp, scale=-0.5),
                  reads=[(("rstd", ri), 0, 512)], writes=[(("rstd", ri), 0, 512)])
            norm_state[(l, k, g, final)] = ri
          if part in ("all", "mod"):
            ri = norm_state[(l, k, g, final)]
            R = rstd[ri].ap()[:, 0:n]
            for dc in range(NDC):
                xs = xT.ap()[:, dc, t0:t0 + n]
                xr = [(("xT", dc), t0, t0 + n)]
                if final:
                    S.add("dve", lambda e, xs=xs, R=R, dc=dc: e.scalar_tensor_tensor(
                        out=xs, in0=xs, scalar=vcol(48 + dc), in1=R, op0=ALU.mult, op1=ALU.mult),
                        reads=xr + [(("rstd", ri), 0, 512), ("vecs", 0, 152)], writes=xr)
                    continue
                if g < 4:
                    c0 = 16 + t0
                    hs = hT.ap()[:, dc, c0:c0 + n]
                    hreg = [(("hT", dc), c0, c0 + n)]
                    S.add("dve", lambda e, hs=hs, xs=xs, R=R, dc=dc: e.scalar_tensor_tensor(
                        out=hs, in0=xs, scalar=modT.ap()[:, l, DER[vA], dc, 0:1], in1=R, op0=ALU.mult, op1=ALU.mult),
                        reads=xr + [(("rstd", ri), 0, 512), (("modT", l, DER[vA]), dc, dc + 1)], writes=hreg)
                    if shift_eng == "act":
                        S.add("act", lambda e, hs=hs, dc=dc: e.activation(
                            out=hs, in_=hs, func=AF.Identity, bias=modT.ap()[:, l, vsh, dc, 0:1], scale=1.0),
                            reads=hreg + [(("modT", l, vsh), dc, dc + 1)], writes=hreg)
                    else:
                        S.add("dve", lambda e, hs=hs, dc=dc: e.tensor_scalar(
                            out=hs, in0=hs, scalar1=modT.ap()[:, l, vsh, dc, 0:1], scalar2=None, op0=ALU.add),
                            reads=hreg + [(("modT", l, vsh), dc, dc + 1)], writes=hreg)
                else:
                    tmp = rstd[ri].ap()[:, 128:256].rearrange("p (b t) -> p b t", t=8)
                    treg = [(("rstd", ri), 0, 512)]
                    x3 = xs.rearrange("p (b t) -> p b t", t=8)
                    R3 = R.rearrange("p (b t) -> p b t", t=8)
                    S.add("dve", lambda e, tmp=tmp, x3=x3, dc=dc: e.tensor_tensor(
                        out=tmp, in0=x3, in1=bc_seq(modT.ap(), l, DER[vA], dc), op=ALU.mult),
                        reads=xr + [(("modT", l, DER[vA]), dc, dc + 1)], writes=treg)
                    S.add("dve", lambda e, tmp=tmp, R3=R3: e.tensor_tensor(out=tmp, in0=tmp, in1=R3, op=ALU.mult),
                          reads=treg + [(("rstd", ri), 0, 512)], writes=treg)
                    if pool_layout:
                        hs3 = hT.ap()[:, dc, SOFF:SOFF + 368].rearrange("p (b j) -> p b j", j=23)[:, :, 15:23]
                        hreg = [(("hT", dc), SOFF, SOFF + 368)]
                        h32 = scr.ap()[:, dc * 128:(dc + 1) * 128].rearrange("p (b t) -> p b t", t=8)
                        S.add("dve", lambda e, tmp=tmp, h32=h32, dc=dc: e.tensor_tensor(
                            out=h32, in0=tmp, in1=bc_seq(modT.ap(), l, vsh, dc), op=ALU.add),
                            reads=treg + [(("modT", l, vsh), dc, dc + 1)], writes=[SCR(dc * 512, (dc + 1) * 512)])
                        S.add("dve", lambda e, hs3=hs3, h32=h32: e.tensor_copy(out=hs3, in_=h32),
                              reads=[SCR(dc * 512, (dc + 1) * 512)], writes=hreg)
                    else:
                        hs3 = hT.ap()[:, dc, SOFF:SOFF + 128].rearrange("p (b t) -> p b t", t=8)
                        hreg = [(("hT", dc), SOFF, SOFF + 128)]
                        S.add("dve", lambda e, tmp=tmp, hs3=hs3, dc=dc: e.tensor_tensor(
                            out=hs3, in0=tmp, in1=bc_seq(modT.ap(), l, vsh, dc), op=ALU.add),
                            reads=treg + [(("modT", l, vsh), dc, dc + 1)], writes=hreg)

    def residual(l, kG, ps_ap, dout_c, g):
        t0, n = GROUPS[g]
        xs = xT.ap()[:, dout_c, t0:t0 + n]
        xr = [(("xT", dout_c), t0, t0 + n)]
        return xs, xr

    def add_residual(l, kG, bank_key, ps_ap, dout_c, g):
        t0, n = GROUPS[g]
        xs = xT.ap()[:, dout_c, t0:t0 + n]
        xr = [(("xT", dout_c), t0, t0 + n)]
        if g < 4:
            S.add("dve", lambda e: e.scalar_tensor_tensor(
                out=xs, in0=ps_ap, scalar=modT.ap()[:, l, DER[kG], dout_c, 0:1], in1=xs, op0=ALU.mult, op1=ALU.add),
                reads=xr + [bank_key, (("modT", l, DER[kG]), dout_c, dout_c + 1)], writes=xr)
        else:
            tmp = scr.ap()[:, 3840:3968].rearrange("p (b t) -> p b t", t=8)
            treg = [SCR(15360, 15872)]
            S.add("dve", lambda e: e.tensor_tensor(out=tmp, in0=ps_ap.rearrange("p (b t) -> p b t", t=8),
                                                   in1=bc_seq(modT.ap(), l, DER[kG], dout_c), op=ALU.mult),
                  reads=[bank_key, (("modT", l, DER[kG]), dout_c, dout_c + 1)], writes=treg)
            S.add("dve", lambda e: e.tensor_tensor(out=xs.rearrange("p (b t) -> p b t", t=8),
                                                   in0=xs.rearrange("p (b t) -> p b t", t=8), in1=tmp, op=ALU.add),
                  reads=xr + treg, writes=xr)

    def mlp(l, after_blk=None, after_last=None, lazy=(), flush=True):
        steps = [(blk, g) for blk in range(8) for g in range(5)]
        pending = []
        lazy = list(lazy)

        def mm1(si):
            blk, g = steps[si]
            c0, n = hcols(g)
            p1 = pidx[("w1", l, blk)]
            wv = pview(p1)
            ab = si % 2
            for fc in range(4):
                if 1 <= blk <= 6 or (blk == 0 and si >= 4):
                    bank, bkey = [(psA[0], ("psA", 0)), (psA[1], ("psA", 1)), (psA[2], ("psA", 2)), (psS, "psS")][nxt("A4m", 4)]
                else:
                    bi = nxt("A", 3)
                    bank, bkey = psA[bi], ("psA", bi)
                pso = bank.ap()[:, 0:n]
                for dc in range(NDC):
                    S.add("pe", lambda e, pso=pso, wv=wv, dc=dc, fc=fc, c0=c0, n=n: e.matmul(
                        pso, wv[:, dc, fc * 128:(fc + 1) * 128], hT.ap()[:, dc, c0:c0 + n],
                        start=(dc == 0), stop=(dc == NDC - 1)),
                        reads=[preg(p1), (("hT", dc), c0, c0 + n)], writes=[(bkey, 0, 512)])
                av = aT[ab].ap()[:, fc, 0:n]
                areg = [(("aT", ab), fc * 512, (fc + 1) * 512)]
                S.add("act", lambda e, av=av, pso=pso: e.activation(out=av, in_=pso, func=AF.Relu),
                      reads=[(bkey, 0, 512)], writes=areg)
                S.add("act", lambda e, av=av: e.activation(out=av, in_=av, func=AF.Square),
                      reads=areg, writes=areg)
            if g == 4:
                done_piece(p1)
                if after_blk is not None:
                    after_blk(blk)

        def mm2(si):
            blk, g = steps[si]
            t0, n = GROUPS[g]
            p2 = pidx[("w2", l, blk)]
            wv = pview(p2)
            ab = si % 2
            for dc in range(NDC):
                bank, bkey = [(psB[0], ("psB", 0)), (psB[1], ("psB", 1)), (psB[2], ("psB", 2)), (psM, "psM")][nxt("B4", 4)]
                pso = bank.ap()[:, 0:n]
                for fc in range(4):
                    S.add("pe", lambda e, pso=pso, wv=wv, dc=dc, fc=fc, n=n, ab=ab: e.matmul(
                        pso, wv[:, fc, dc * 128:(dc + 1) * 128], aT[ab].ap()[:, fc, 0:n],
                        start=(fc == 0), stop=(fc == 3)),
                        reads=[preg(p2), (("aT", ab), fc * 512, (fc + 1) * 512)], writes=[(bkey, 0, 512)])
                add_residual(l, 3, (bkey, 0, 512), pso, dc, g)
            if g == 4:
                done_piece(p2)
            if blk == 7 and after_last is not None:
                pending.append((g + 1, lambda g=g: after_last(g, "stats")))
                pending.append((g + 2, lambda g=g: after_last(g, "mod")))
                pending.sort(key=lambda t: t[0])

        for i in range(len(steps) + 1):
            if i < len(steps):
                mm1(i)
                if lazy and i < 3:
                    lazy.pop(0)()
                    while i == 2 and lazy:
                        lazy.pop(0)()
            if i >= 1:
                bprev, gprev = steps[i - 1]
                if bprev == 7:
                    while pending and pending[0][0] <= gprev:
                        pending.pop(0)[1]()
                mm2(i - 1)
        rest_ = [f_ for (_, f_) in pending]
        if flush:
            for f_ in rest_:
                f_()
            return []
        return rest_

    def pool_mixer(after_group=None, mid_group=None):
        l = 0
        wv = aux4k.ap().rearrange("p (a b) -> p a b", b=256)
        Pall = [scr_bf[:, 2048 + i * 528: 2048 + (i + 1) * 528] for i in range(4)]
        Pregall = [SCR(4096 + i * 1056, 4096 + (i + 1) * 1056) for i in range(4)]
        Wn = scr_bf[:, 4160:6208].rearrange("p (a b) -> p a b", b=256)
        for k_ in range(8):
            S.add("act", lambda e, k_=k_: e.activation(out=Wn[:, k_, :], in_=wv[:, k_, :], func=AF.Identity, scale=-float(2 ** (k_ // 2 + 1))),
                  reads=[("aux4k", 0, 1)], writes=[SCR(8320 + k_ * 512, 8320 + (k_ + 1) * 512)])
        dT = [aT[0].ap()[:, i, :] for i in range(4)] + [aT[1].ap()[:, i, :] for i in range(4)]
        dreg = [[(("aT", 0), i * 512, (i + 1) * 512)] for i in range(4)] + [[(("aT", 1), i * 512, (i + 1) * 512)] for i in range(4)]
        for g, (t0, n) in enumerate(GROUPS):
            for dc in range(NDC):
                gi = dc // 2
                w = 2 ** (gi + 1)
                hk = ("hT", dc)
                en = "dve"
                P = Pall[0:2]
                Preg = Pregall[0:2]
                if g < 4:
                    c0 = 16 + t0

                    def hv(sh, lo=0, hi=n, dc=dc, c0=c0):
                        return hT.ap()[:, dc, c0 + lo - sh:c0 + hi - sh]
                    hreg = [(hk, c0 - 16, c0 + n)]
                    dv = dT[dc][:, 0:n]
                    if gi == 0:
                        S.add("dve", lambda e, dv=dv, hv=hv: e.tensor_tensor(out=dv, in0=hv(1), in1=hv(0), op=ALU.add),
                              reads=hreg, writes=dreg[dc])
                    else:
                        ext = n + 16
                        if w == 4:
                            pass
                        S.add(en, lambda e, dc=dc, c0=c0, ext=ext, P=P: e.tensor_tensor(
                            out=P[0][:, 1:ext], in0=hT.ap()[:, dc, c0 - 15:c0 + ext - 16],
                            in1=hT.ap()[:, dc, c0 - 16:c0 + ext - 17], op=ALU.add),
                            reads=hreg, writes=[Preg[0]])
                        cur, lev, lo = 0, 2, 1
                        while lev * 2 < w:
                            nlo = lo + lev
                            S.add(en, lambda e, cur=cur, lev=lev, nlo=nlo, ext=ext, P=P: e.tensor_tensor(
                                out=P[1 - cur][:, nlo:ext], in0=P[cur][:, nlo:ext], in1=P[cur][:, nlo - lev:ext - lev],
                                op=ALU.add), reads=[Preg[cur]], writes=[Preg[1 - cur]])
                            cur, lev, lo = 1 - cur, lev * 2, nlo
                        S.add(en, lambda e, dv=dv, cur=cur, lev=lev, n=n, P=P: e.tensor_tensor(
                            out=dv, in0=P[cur][:, 16:16 + n], in1=P[cur][:, 16 - lev:16 + n - lev], op=ALU.add),
                            reads=[Preg[cur]], writes=dreg[dc])
                    if g == 0:
                        S.add(en, lambda e, dc=dc, gi=gi: e.tensor_tensor(
                            out=dT[dc][:, 0:16], in0=dT[dc][:, 0:16], in1=rvec.ap()[:, gi, :], op=ALU.mult),
                            reads=dreg[dc] + [("rvec", 0, 1)], writes=dreg[dc])
                else:
                    H = hT.ap()[:, dc, SOFF:SOFF + 368].rearrange("p (b j) -> p b j", j=23)
                    hreg = [(hk, SOFF, SOFF + 368)]
                    dv = dT[dc][:, 0:128].rearrange("p (b t) -> p b t", t=8)
                    if gi == 0:
                        S.add("dve", lambda e, dv=dv, H=H: e.tensor_tensor(out=dv, in0=H[:, :, 14:22], in1=H[:, :, 15:23],
                                                                           op=ALU.subtract), reads=hreg, writes=dreg[dc])
                    else:
                        Pv = [P[i][:, 0:368].rearrange("p (b j) -> p b j", j=23) for i in range(2)]
                        S.add(en, lambda e, Pv=Pv, H=H: e.tensor_tensor(out=Pv[0][:, :, 1:23], in0=H[:, :, 1:23],
                                                                           in1=H[:, :, 0:22], op=ALU.add),
                              reads=hreg, writes=[Preg[0]])
                        cur, lev, lo = 0, 2, 1
                        while lev < w:
                            nlo = lo + lev
                            S.add(en, lambda e, Pv=Pv, cur=cur, lev=lev, nlo=nlo: e.tensor_tensor(
                                out=Pv[1 - cur][:, :, nlo:23], in0=Pv[cur][:, :, nlo:23], in1=Pv[cur][:, :, nlo - lev:23 - lev],
                                op=ALU.add), reads=[Preg[cur]], writes=[Preg[1 - cur]])
                            cur, lev, lo = 1 - cur, lev * 2, nlo
                        S.add("dve", lambda e, dv=dv, H=H, Pv=Pv, cur=cur, w=w: e.scalar_tensor_tensor(
                            out=dv, in0=H[:, :, 15:23], scalar=-float(w), in1=Pv[cur][:, :, 15:23], op0=ALU.mult, op1=ALU.add),
                            reads=hreg + [Preg[cur]], writes=dreg[dc])
            if mid_group is not None:
                mid_group(g)
            for dcout in range(NDC):
                gi, oc = dcout // 2, dcout % 2
                bi = nxt("B", 3)
                pso = psB[bi].ap()[:, 0:n]
                for cc in range(2):
                    k_ = gi * 2 + cc
                    S.add("pe", lambda e, pso=pso, k_=k_, oc=oc, cc=cc, n=n, g=g: e.matmul(
                        pso, wv[:, k_, oc * 128:(oc + 1) * 128], dT[k_][:, 0:n],
                        start=(cc == 0), stop=(cc == 1 and g == 4)),
                        reads=[("aux4k", 0, 1)] + dreg[k_], writes=[(("psB", bi), 0, 512)])
                if g < 4:
                    for cc in range(2):
                        k_ = gi * 2 + cc
                        S.add("pe", lambda e, pso=pso, k_=k_, oc=oc, cc=cc, n=n, t0=t0: e.matmul(
                            pso, Wn[:, k_, oc * 128:(oc + 1) * 128], hT.ap()[:, k_, 16 + t0:16 + t0 + n],
                            start=False, stop=(cc == 1)),
                            reads=[SCR(8320 + k_ * 512, 8320 + (k_ + 1) * 512), (("hT", k_), 16 + t0, 16 + t0 + n)],
                            writes=[(("psB", bi), 0, 512)])
                add_residual(0, 2, (("psB", bi), 0, 512), pso, dcout, g)
            if after_group is not None:
                after_group(g)

    def pool_outputs():
        l, k = 0, 0
        pass

    def sgu_setup():
        w32 = scr.ap()[:, 0:1024].rearrange("p (g k t) -> p g k t", g=4, k=2)
        m32 = scr.ap()[:, 1024:1280].rearrange("p (k t) -> p k t", k=2)
        S.add("sp", lambda e: e.dma_start(out=w32, in_=d_wsp.ap()), writes=[SCR(0, 4096)], stream="s_wsp")
        S.add("sp", lambda e: e.dma_start(out=m32, in_=d_msk.ap()), writes=[SCR(4096, 5120)], stream="s_msk")
        for gi in range(4):
            S.add("dve", lambda e, gi=gi: e.tensor_tensor(out=wsp.ap()[:, gi, :, :], in0=w32[:, gi, :, :], in1=m32, op=ALU.mult),
                  reads=[SCR(0, 5120)], writes=[("wsp", gi, gi + 1)])
        S.add("pool", lambda e: e.dma_start(out=binv.ap(), in_=d_binv.ap()), writes=[("binv", 0, 1)], stream="s_binv")

    def sgu(after_sub=None, lazy=()):
        l = 1
        lazy = list(lazy)
        tail = []
        deferred = []
        banks4 = [(psA[0], ("psA", 0)), (psA[1], ("psA", 1)), (psA[2], ("psA", 2)), (psM, "psM")]
        rot["A4"] = 0
        vtm = scr_bf[:, 0:4096].rearrange("p (t c) -> p t c", t=4)
        vst = scr.ap()[:, 2048:3072]
        gbc = scr.ap()[:, 3072:4096]
        bbc = aux4k.ap().bitcast(F32)
        VST = SCR(8192, 12288)
        S.add("sp", lambda e: e.dma_start(out=gbc, in_=bass.AP(d_lng, 0, [[0, 128], [1, D]])), writes=[SCR(12288, 16384)], stream="s_lng")
        S.add("sp", lambda e: e.dma_start(out=bbc, in_=bass.AP(d_lnb, 0, [[0, 128], [1, D]])), writes=[("aux4k", 0, 1)], stream="s_lnb")
        pu = [pidx[("inu", 1, h)] for h in range(2)]
        pv_ = [pidx[("inv", 1, h)] for h in range(2)]
        po = [pidx[("out", 1, h)] for h in range(2)]
        L = lnst.ap()
        S.add("sp", lambda e: e.dma_start(out=btile.ap(), in_=bass.AP(d_bsp, 0, [[0, 128], [256, 4], [1, 128]])),
              writes=[("btile", 0, 1)], stream="s_bt")
        for g, (t0, n) in enumerate(GROUPS):
            c0, _ = hcols(g)
            sample = (g == 4)
            last = (g == len(GROUPS) - 1)
            ntile = n // 128
            if sample:
                uT = [scr_bf[:, 1024 + fc * 128:1024 + (fc + 1) * 128] for fc in range(NDC)]
                ureg = [[SCR(2048 + fc * 256, 2048 + (fc + 1) * 256)] for fc in range(NDC)]
            else:
                uT = [aT[fc // 4].ap()[:, fc % 4, 0:n] for fc in range(NDC)]
                ureg = [[(("aT", fc // 4), (fc % 4) * 512, (fc % 4 + 1) * 512)] for fc in range(NDC)]
            def emit_u(fcs, c0=c0, n=n, uT=uT, ureg=ureg):
                for fc in fcs:
                    wv = pview(pu[fc // 4])
                    bank, bkey = banks4[nxt("A4", 4)]
                    pso = bank.ap()[:, 0:n]
                    for dc in range(NDC):
                        S.add("pe", lambda e, pso=pso, wv=wv, dc=dc, fc=fc: e.matmul(
                            pso, wv[:, dc, (fc % 4) * 128:(fc % 4 + 1) * 128], hT.ap()[:, dc, c0:c0 + n],
                            start=(dc == 0), stop=(dc == NDC - 1)),
                            reads=[preg(pu[fc // 4]), (("hT", dc), c0, c0 + n)], writes=[(bkey, 0, 512)])
                    S.add("act", lambda e, pso=pso, fc=fc: e.activation(out=uT[fc], in_=pso, func=AF.Gelu,
                                                                        bias=vcol(40 + fc), scale=1.0),
                          reads=[(bkey, 0, 512), ("vecs", 0, 152)], writes=ureg[fc])
            for tt in range(ntile):
                out_tile = (t0 + tt * 128 == TP - 128) or sample
                for hf in range(2):
                    wv = pview(pv_[hf])
                    bank, bkey = banks4[nxt("A4", 4)]
                    pso = bank.ap()
                    for dc in range(NDC):
                        S.add("pe", lambda e, pso=pso, wv=wv, dc=dc, c0=c0, tt=tt: e.matmul(
                            pso, hT.ap()[:, dc, c0 + tt * 128:c0 + (tt + 1) * 128], wv[:, dc, :], start=(dc == 0), stop=False),
                            reads=[preg(pv_[hf]), (("hT", dc), c0, c0 + n)], writes=[(bkey, 0, 512)])
                    S.add("pe", lambda e, pso=pso, hf=hf: e.matmul(pso, ones.ap()[0:1, :], binv.ap()[0:1, hf * 512:(hf + 1) * 512],
                                                                   start=False, stop=True),
                          reads=[("ones", 0, 1), ("binv", 0, 1)], writes=[(bkey, 0, 512)])
                    S.add("act", lambda e, pso=pso, hf=hf: e.activation(out=vst[:, hf * 512:(hf + 1) * 512], in_=pso, func=AF.Gelu,
                                                                        accum_out=lnst.ap()[:, hf:hf + 1]),
                          reads=[(bkey, 0, 512)], writes=[SCR(8192 + hf * 2048, 8192 + (hf + 1) * 2048), ("lnst", hf, hf + 1)])
                    si2 = nxt("sq", 7)
                    S.add("act", lambda e, hf=hf, si2=si2: e.activation(out=sq[si2].ap(), in_=vst[:, hf * 512:(hf + 1) * 512], func=AF.Square,
                                                                       accum_out=lnst.ap()[:, 2 + hf:3 + hf]),
                          reads=[SCR(8192 + hf * 2048, 8192 + (hf + 1) * 2048)], writes=[(("sq", si2), 0, 512), ("lnst", 2 + hf, 3 + hf)])
                if last and tt == ntile - 1:
                    done_piece(pv_[0]); done_piece(pv_[1])
                S.add("dve", lambda e: e.tensor_tensor(out=L[:, 4:5], in0=L[:, 0:1], in1=L[:, 1:2], op=ALU.add),
                      reads=[("lnst", 0, 2)], writes=[("lnst", 4, 5)])
                S.add("dve", lambda e: e.tensor_tensor(out=L[:, 5:6], in0=L[:, 2:3], in1=L[:, 3:4], op=ALU.add),
                      reads=[("lnst", 2, 4)], writes=[("lnst", 5, 6)])
                S.add("dve", lambda e: e.tensor_scalar(out=L[:, 4:6], in0=L[:, 4:6], scalar1=1.0 / D, scalar2=None, op0=ALU.mult),
                      reads=[("lnst", 4, 6)], writes=[("lnst", 4, 6)])
                S.add("dve", lambda e: e.tensor_tensor(out=L[:, 6:7], in0=L[:, 4:5], in1=L[:, 4:5], op=ALU.mult),
                      reads=[("lnst", 4, 5)], writes=[("lnst", 6, 7)])
                S.add("dve", lambda e: e.scalar_tensor_tensor(out=L[:, 7:8], in0=L[:, 5:6], scalar=float(EPS), in1=L[:, 6:7],
                                                              op0=ALU.add, op1=ALU.subtract),
                      reads=[("lnst", 5, 7)], writes=[("lnst", 7, 8)])
                S.add("pool", lambda e: e.tensor_tensor(out=L[:, 8:9], in0=L[:, 7:8], in1=mhalf.ap(), op=ALU.pow),
                      reads=[("lnst", 7, 8), ("mhalf", 0, 1)], writes=[("lnst", 8, 9)])
                vreg = SCR(tt * 2048, (tt + 1) * 2048)
                if not out_tile:
                    S.add("dve", lambda e, tt=tt: e.tensor_scalar(out=vtm[:, tt, :], in0=vst, scalar1=L[:, 4:5], scalar2=L[:, 8:9],
                                                               op0=ALU.subtract, op1=ALU.mult),
                          reads=[VST, ("lnst", 4, 9)], writes=[vreg])
                    S.add("dve", lambda e, tt=tt: e.tensor_tensor(out=vtm[:, tt, :], in0=vtm[:, tt, :], in1=gbc, op=ALU.mult),
                          reads=[vreg, SCR(12288, 16384)], writes=[vreg])
                    S.add("dve", lambda e, tt=tt: e.tensor_tensor(out=vtm[:, tt, :], in0=vtm[:, tt, :], in1=bbc, op=ALU.add),
                          reads=[vreg, ("aux4k", 0, 1)], writes=[vreg])
                else:
                    S.add("dve", lambda e: e.tensor_scalar(out=vst, in0=vst, scalar1=L[:, 4:5], scalar2=L[:, 8:9],
                                                           op0=ALU.subtract, op1=ALU.mult),
                          reads=[VST, ("lnst", 4, 9)], writes=[VST])
                    S.add("dve", lambda e: e.tensor_tensor(out=vst, in0=vst, in1=gbc, op=ALU.mult),
                          reads=[VST, SCR(12288, 16384)], writes=[VST])
                if out_tile:
                    S.add("dve", lambda e: e.tensor_tensor(out=vst, in0=vst, in1=bbc, op=ALU.add),
                          reads=[VST, ("aux4k", 0, 1)], writes=[VST])
                    S.add("dve", lambda e, tt=tt: e.tensor_copy(out=vtm[:, tt, :], in_=vst), reads=[VST], writes=[vreg])
                    od = o_vS if sample else o_vP
                    S.add("sp", lambda e, od=od: e.dma_start(out=od.ap(), in_=vst), reads=[VST], stream="o_v%d" % int(sample))
                emit_u(range(8) if ntile == 1 else [[0], [1], [2, 3], [4, 5, 6, 7]][tt])
            if last:
                done_piece(pu[0]); done_piece(pu[1])
            def part_b(g=g, t0=t0, n=n, c0=c0, sample=sample, last=last, ntile=ntile, uT=uT, ureg=ureg):
                kk = 1 if sample else 0
                if sample:
                    S.add("sp", lambda e: e.dma_start(out=btile.ap(), in_=bass.AP(d_bsp, 128, [[0, 128], [256, 4], [1, 128]])),
                          writes=[("btile", 0, 1)], stream="s_bt")
                for cc in range(NDC):
                    gi = cc // 2
                    bank, bkey = banks4[nxt("A4", 4)]
                    for tt in range(ntile):
                        pso = bank.ap()[:, tt * 128:(tt + 1) * 128]
                        S.add("pe", lambda e, pso=pso, tt=tt, cc=cc, gi=gi, kk=kk: e.matmul(
                            pso, vtm[:, tt, cc * 128:(cc + 1) * 128], wsp.ap()[:, gi, kk, :], start=True, stop=True),
                            reads=[SCR(tt * 2048, (tt + 1) * 2048), ("wsp", gi, gi + 1)], writes=[(bkey, 0, 512)])
                    bt = btile.ap()[:, gi, :]
                    btb = bass.AP(bt.tensor, bt.offset, [list(bt.ap[0]), [0, ntile], [1, 128]])
                    p3 = bank.ap()[:, 0:n].rearrange("p (a b) -> p a b", b=128)
                    S.add("dve", lambda e, p3=p3, btb=btb: e.tensor_tensor(out=p3, in0=p3, in1=btb, op=ALU.add),
                          reads=[(bkey, 0, 512), ("btile", 0, 1)], writes=[(bkey, 0, 512)])
                    S.add("dve", lambda e, bank=bank, cc=cc, n=n, uT=uT: e.tensor_tensor(
                        out=uT[cc], in0=bank.ap()[:, 0:n], in1=uT[cc], op=ALU.mult),
                        reads=[(bkey, 0, 512)] + ureg[cc], writes=ureg[cc])
                for dco in range(NDC):
                    wv = pview(po[dco // 4])
                    bi = nxt("B", 3)
                    pso = psB[bi].ap()[:, 0:n]
                    for cc in range(NDC):
                        S.add("pe", lambda e, pso=pso, wv=wv, dco=dco, cc=cc, uT=uT: e.matmul(
                            pso, wv[:, cc, (dco % 4) * 128:(dco % 4 + 1) * 128], uT[cc], start=(cc == 0), stop=(cc == NDC - 1)),
                            reads=[preg(po[dco // 4])] + ureg[cc], writes=[(("psB", bi), 0, 512)])
                    xs = xT.ap()[:, dco, t0:t0 + n]
                    xr = [(("xT", dco), t0, t0 + n)]
                    bk = (("psB", bi), 0, 512)
                    if not sample:
                        S.add("dve", lambda e, xs=xs, pso=pso, dco=dco: e.scalar_tensor_tensor(
                            out=xs, in0=pso, scalar=modT.ap()[:, 1, DER[2], dco, 0:1], in1=xs, op0=ALU.mult, op1=ALU.add),
                            reads=xr + [bk, (("modT", 1, DER[2]), dco, dco + 1)], writes=xr)
                    else:
                        tmp = scr.ap()[:, 1024:1152].rearrange("p (b t) -> p b t", t=8)
                        S.add("dve", lambda e, pso=pso, dco=dco, tmp=tmp: e.tensor_tensor(
                            out=tmp, in0=pso.rearrange("p (b t) -> p b t", t=8), in1=bc_seq(modT.ap(), 1, DER[2], dco), op=ALU.mult),
                            reads=[bk, (("modT", 1, DER[2]), dco, dco + 1)], writes=[SCR(4096, 4608)])
                        S.add("dve", lambda e, xs=xs, tmp=tmp: e.tensor_tensor(
                            out=xs.rearrange("p (b t) -> p b t", t=8), in0=xs.rearrange("p (b t) -> p b t", t=8), in1=tmp, op=ALU.add),
                            reads=xr + [SCR(4096, 4608)], writes=xr)
                if last:
                    done_piece(po[0]); done_piece(po[1])
            if sample:
                deferred.append(part_b)
            else:
                part_b()
            if lazy:
                lazy.pop(0)()
            if after_sub is not None:
                tail.append((g + 1, lambda g=g: after_sub(g, "stats_a")))
                tail.append((g + 2, lambda g=g: (after_sub(g, "stats_b"), after_sub(g, "mod"))))
                tail.sort(key=lambda t: t[0])
                while tail and tail[0][0] <= g:
                    tail.pop(0)[1]()
        rest_ = [f_ for (_, f_) in tail]
        return rest_[:1] + deferred + rest_[1:]


    for a_ in range(4):
        ada_piece(0, a_)
    ada_derive_A(0, 0)
    rest = list(range(4, 12))
    norm_group(0, 0, 0, pool_layout=True, part="stats")
    for g in range(5):
        if g + 1 < 5:
            norm_group(0, 0, g + 1, pool_layout=True, part="stats")
        norm_group(0, 0, g, pool_layout=True, part="mod", shift_eng=("act" if g % 2 == 0 else "dve"))
        for _ in range(2):
            if rest:
                ada_piece(0, rest.pop(0))
    while rest:
        ada_piece(0, rest.pop(0))
    ada_derive_G1(0)
    ada_derive_A(0, 1)
    ada_derive_G2(0)
    hp = scr.ap()[:, 3584:3712].rearrange("p (dc t) -> p dc t", t=16)
    for dc in range(NDC):
        xs = xT.ap()[:, dc, TP - 16:TP]
        S.add("dve", lambda e, dc=dc, xs=xs: e.scalar_tensor_tensor(
            out=hp[:, dc, :], in0=xs, scalar=modT.ap()[:, 0, DER[0], dc, 0:1], in1=rstd[1].ap()[:, 496:512], op0=ALU.mult, op1=ALU.mult),
            reads=[(("xT", dc), TP - 16, TP), (("rstd", 1), 0, 512), (("modT", 0, DER[0]), dc, dc + 1)], writes=[SCR(14336 + dc * 64, 14400 + dc * 64)])
        S.add("dve", lambda e, dc=dc: e.tensor_scalar(out=hp[:, dc, :], in0=hp[:, dc, :], scalar1=modT.ap()[:, 0, 0, dc, 0:1],
                                                      scalar2=None, op0=ALU.add),
              reads=[SCR(14336 + dc * 64, 14400 + dc * 64), (("modT", 0, 0), dc, dc + 1)], writes=[SCR(14336 + dc * 64, 14400 + dc * 64)])
        S.add("sp", lambda e, dc=dc: e.dma_start(out=o_npp.ap()[dc * 128:(dc + 1) * 128, :], in_=hp[:, dc, 1:16]),
              reads=[SCR(14336 + dc * 64, 14400 + dc * 64)], stream="o_npp")
        h32 = scr.ap()[:, dc * 128:(dc + 1) * 128].rearrange("p (b t) -> p b t", t=8)
        S.add("sp", lambda e, dc=dc, h32=h32: e.dma_start(out=o_npsb.ap()[dc * 128:(dc + 1) * 128, :].rearrange("p (b t) -> p b t", t=8), in_=h32),
              reads=[SCR(dc * 512, (dc + 1) * 512)], stream="o_nps1")

    ada1 = list(range(12))

    def pm_mid(g):
        if 1 <= g <= 3:
            norm_group(0, 1, g - 1, shift_eng="act")

    def pm_after(g):
        for _ in range(3):
            if ada1:
                ada_piece(1, ada1.pop(0))
    pool_mixer(after_group=pm_after, mid_group=pm_mid)
    while ada1:
        ada_piece(1, ada1.pop(0))
    ada_derive_A(1, 0)
    ada_derive_G1(1)
    ada_derive_A(1, 1)
    ada_derive_G2(1)

    def mlp0_after_blk(blk):
        if blk == 6:
            sgu_setup()
    tail0 = mlp(0, after_blk=mlp0_after_blk, after_last=lambda g, part: norm_group(1, 0, g, part=part),
                lazy=[lambda: norm_group(0, 1, 3, part="stats"),
                      lambda: (norm_group(0, 1, 3, part="mod"), norm_group(0, 1, 4, part="stats")),
                      lambda: norm_group(0, 1, 4, part="mod")], flush=False)

    tail1 = sgu(after_sub=lambda g, part: norm_group(1, 1, g, part=part), lazy=tail0)

    def final_group(g, part):
        norm_group(0, 0, g, final=True, part=part)
        if part == "stats":
            return
        t0, n = GROUPS[g]
        if g >= 3:
            S.add("sp", lambda e: e.dma_start(out=o_yT.ap()[:, t0:t0 + n].rearrange("(dc p) t -> p dc t", p=128),
                                              in_=xT.ap()[:, :, t0:t0 + n]),
                  reads=[(("xT", dc), t0, t0 + n) for dc in range(NDC)], stream="o_yg%d" % g)
            return
        for dc in range(NDC):
            S.add("sp", lambda e, dc=dc: e.dma_start(out=o_yT.ap()[dc * 128:(dc + 1) * 128, t0:t0 + n], in_=xT.ap()[:, dc, t0:t0 + n]),
                  reads=[(("xT", dc), t0, t0 + n)], stream="o_y%d" % dc)
    mlp(1, after_last=final_group, lazy=tail1)
    S.emit_all()
    print("sched stats:", S.stats)
    return nc


_CACHE = {}


def kernel(x_prompt, x_sample, c_prompt, c_sample, state_pool, norm_g, w_ada, b_ada, w_pool, pool_scale,
           sgu_w_in, sgu_b_in, sgu_ln_g, sgu_ln_b, sgu_w_sp, sgu_b_sp, sgu_w_out, mlp_w1, mlp_w2, final_g):
    f = lambda a: np.ascontiguousarray(np.asarray(a), dtype=np.float32)
    x_prompt, x_sample, c_prompt, c_sample, state_pool = map(f, (x_prompt, x_sample, c_prompt, c_sample, state_pool))
    norm_g, w_ada, b_ada, w_pool, pool_scale = map(f, (norm_g, w_ada, b_ada, w_pool, pool_scale))
    sgu_w_in, sgu_b_in, sgu_ln_g, sgu_ln_b, sgu_w_sp, sgu_b_sp, sgu_w_out = map(
        f, (sgu_w_in, sgu_b_in, sgu_ln_g, sgu_ln_b, sgu_w_sp, sgu_b_sp, sgu_w_out))
    mlp_w1, mlp_w2, final_g = map(f, (mlp_w1, mlp_w2, final_g))

    fm = lambda v: v.reshape(-1, 128).T
    vecs = np.concatenate([fm(norm_g.reshape(-1)), fm(pool_scale.reshape(-1)), fm(sgu_b_in[0, :D]), fm(final_g),
                           fm(b_ada.reshape(-1))], axis=1)
    assert vecs.shape == (128, 152)
    wsp_pack = np.zeros((128, 4, 2, 128), np.float32)
    bsp_pack = np.zeros((1, 4, 2, 128), np.float32)
    for gi in range(4):
        wsp_pack[:, gi, 0, :] = sgu_w_sp[0, gi].T
        wsp_pack[:, gi, 1, :] = np.tile(sgu_w_sp[0, gi][:8, :8].T, (16, 16))
        bsp_pack[0, gi, 0, :] = sgu_b_sp[0, gi]
        bsp_pack[0, gi, 1, :] = np.tile(sgu_b_sp[0, gi][:8], 16)
    s_idx = np.arange(128)[:, None]
    t_idx = np.arange(128)[None, :]
    msk = np.zeros((128, 2, 128), np.float32)
    msk[:, 0, :] = (s_idx <= t_idx)
    msk[:, 1, :] = (s_idx <= t_idx) & ((s_idx // 8) == (t_idx // 8))
    rv = np.ones((128, 4, 16), np.float32)
    for gi in range(4):
        w = 2 ** (gi + 1)
        rv[:, gi, :] = (w / np.minimum(w, np.arange(16) + 1.0))[None, :]

    shared = {
        "vecs": np.ascontiguousarray(vecs), "w_ada": w_ada, "w_pool": w_pool[0], "sgu_w_in": sgu_w_in[0],
        "sgu_w_out": sgu_w_out[0], "mlp_w1": mlp_w1, "mlp_w2": mlp_w2, "wsp_pack": wsp_pack, "msk_pack": msk,
        "bsp_pack": bsp_pack, "binv": np.ascontiguousarray(sgu_b_in[:, D:]), "ln_g": sgu_ln_g, "ln_b": sgu_ln_b, "rvec": rv,
    }
    in_maps = []
    for i in range(8):
        xs = x_sample[16 * i:16 * i + 16].reshape(128, D)
        m = dict(shared)
        m["xT"] = np.ascontiguousarray(np.concatenate([x_prompt[i].T, xs.T], axis=1))
        m["cT"] = np.ascontiguousarray(np.concatenate([c_prompt[i][:, None], c_sample[16 * i:16 * i + 16].T], axis=1))
        spt = state_pool[0, 16 * i:16 * i + 16].transpose(2, 0, 1)
        m["spT"] = np.ascontiguousarray(np.pad(spt, ((0, 0), (0, 0), (0, 8))).reshape(D, 368))
        m["sp_tail"] = np.ascontiguousarray(spt[:, :, 8:15].reshape(D, 112))
        in_maps.append(m)

    if "nc" not in _CACHE:
        _CACHE["nc"] = build_program()
    res = run_bass_kernel_spmd(_CACHE["nc"], in_maps, core_ids=list(range(8)))
    R = res.results
    y_prompt = np.stack([R[i]["yT"][:, :TP].T for i in range(8)]).astype(np.float32)
    y_sample = np.concatenate([R[i]["yT"][:, TP:].T.reshape(16, 8, D) for i in range(8)]).astype(np.float32)
    npp = np.stack([R[i]["nppT"].T for i in range(8)])[None].astype(np.float32)
    nps = np.concatenate([np.concatenate([R[i]["npsTa"].reshape(D, 16, 7), R[i]["npsTb"].reshape(D, 16, 8)], axis=2).transpose(1, 2, 0)
                          for i in range(8)])[None].astype(np.float32)
    vP = np.stack([R[i]["vP"] for i in range(8)])[None].astype(np.float32)
    vS = np.concatenate([R[i]["vS"].reshape(16, 8, D) for i in range(8)])[None].astype(np.float32)
    return (np.ascontiguousarray(y_prompt), np.ascontiguousarray(y_sample), np.ascontiguousarray(npp),
            np.ascontiguousarray(nps), np.ascontiguousarray(vP), np.ascontiguousarray(vS))
```
